# Optimizing a Trainium2 kernel written in Bass

```python
import math
import jax, jax.numpy as jnp
from jax import lax
import numpy as np

D_MODEL = 1024
BATCH = 16
SEQ = 2048
DEPTH = 1

PLE_DIM = 256
RWKV_HEAD_DIM = 64
RWKV_WIDTH = D_MODEL // 2
RWKV_HEADS = RWKV_WIDTH // RWKV_HEAD_DIM
DECAY_LORA = 64
AAA_LORA = 64
GATE_LORA = 128
POOL_WINDOWS = (2, 4, 8, 16)
POOL_GROUPS = len(POOL_WINDOWS)
POOL_WIDTH = D_MODEL // 2
POOL_GROUP_DIM = POOL_WIDTH // POOL_GROUPS
N_BRANCHES = 2
D_FF = 4 * D_MODEL
RMS_EPS = 1e-6
GN_EPS = 64e-5
L2_EPS = 1e-12

RWKV_COLS = 3 * RWKV_WIDTH + DECAY_LORA + AAA_LORA + GATE_LORA
GATE_COLS = N_BRANCHES * D_MODEL
D_IN = RWKV_COLS + POOL_WIDTH + GATE_COLS

kernel_name = "hybrid_rwkv7_multipool_gated"


def rmsnorm(x, g):
    xf = x.astype(jnp.float32)
    y = xf * lax.rsqrt(jnp.mean(xf * xf, axis=-1, keepdims=True) + RMS_EPS)
    return (y * g.astype(jnp.float32)).astype(x.dtype)


def token_shift(z, mu):
    z_prev = jnp.pad(z[:, :-1], ((0, 0), (1, 0), (0, 0)))
    return z + (z_prev - z) * mu


def rwkv7_step(S, inp):
    r, w, k, v, kk, a = inp
    sa = jnp.einsum('bhvk,bhk->bhv', S, -kk)
    S = (S * w[:, :, None, :]
         + sa[..., None] * (kk * a)[:, :, None, :]
         + v[..., None] * k[:, :, None, :])
    y = jnp.einsum('bhvk,bhk->bhv', S, r)
    return S, y


def rwkv7_mix(zs, w0, w_decay_up, a0, w_aaa_up, w_gate_up, k_k, k_a, r_k, ln_w, ln_b):
    B, T, _ = zs.shape
    H, N = RWKV_HEADS, RWKV_HEAD_DIM
    zf = zs.astype(jnp.float32)
    o = 0
    r = zf[..., o:o + RWKV_WIDTH]; o += RWKV_WIDTH
    k = zf[..., o:o + RWKV_WIDTH]; o += RWKV_WIDTH
    v = zf[..., o:o + RWKV_WIDTH]; o += RWKV_WIDTH
    xw = zf[..., o:o + DECAY_LORA]; o += DECAY_LORA
    xa = zf[..., o:o + AAA_LORA]; o += AAA_LORA
    xg = zf[..., o:o + GATE_LORA]
    f32 = lambda t: t.astype(jnp.float32)

    w_log = -jax.nn.softplus(-(f32(w0) + jnp.tanh(xw) @ f32(w_decay_up))) - 0.5
    decay = jnp.exp(-jnp.exp(w_log))
    a = jax.nn.sigmoid(f32(a0) + xa @ f32(w_aaa_up))
    g = jax.nn.sigmoid(xg) @ f32(w_gate_up)

    heads = lambda t: t.reshape(B, T, H, N)
    kk = heads(k * f32(k_k))
    kk = kk / jnp.maximum(jnp.linalg.norm(kk, axis=-1, keepdims=True), L2_EPS)
    k = k * (1.0 + (a - 1.0) * f32(k_a))
    r_h, w_h, k_h, v_h, a_h = heads(r), heads(decay), heads(k), heads(v), heads(a)

    tm = lambda t: jnp.moveaxis(t, 1, 0)
    S0 = jnp.zeros((B, H, N, N), jnp.float32)
    _, y = lax.scan(rwkv7_step, S0,
                    (tm(r_h), tm(w_h), tm(k_h), tm(v_h), tm(kk), tm(a_h)))
    y = jnp.moveaxis(y, 0, 1)

    mean = jnp.mean(y, axis=-1, keepdims=True)
    var = jnp.mean(jnp.square(y - mean), axis=-1, keepdims=True)
    y = (y - mean) * lax.rsqrt(var + GN_EPS)
    y = y * f32(ln_w).reshape(H, N) + f32(ln_b).reshape(H, N)
    bonus = jnp.sum(r_h * k_h * f32(r_k), axis=-1, keepdims=True) * v_h
    out = (y + bonus).reshape(B, T, RWKV_WIDTH) * g
    return out.astype(zs.dtype)


def multiscale_pool_mix(z, pool_w, pool_scale):
    B, T, _ = z.shape
    zg = z.astype(jnp.float32).reshape(B, T, POOL_GROUPS, POOL_GROUP_DIM)
    cs = jnp.cumsum(zg, axis=1)
    pos = jnp.arange(T)
    outs = []
    for gi, win in enumerate(POOL_WINDOWS):
        c = cs[:, :, gi]
        c_prev = jnp.pad(c, ((0, 0), (win, 0), (0, 0)))[:, :T]
        count = jnp.minimum(pos + 1, win).astype(jnp.float32)
        outs.append((c - c_prev) / count[None, :, None] - zg[:, :, gi])
    pooled = jnp.stack(outs, axis=2)
    mixed = jnp.einsum('btgi,gio->btgo', pooled, pool_w.astype(jnp.float32))
    mixed = mixed.reshape(B, T, POOL_WIDTH) * pool_scale.astype(jnp.float32)
    return mixed.astype(z.dtype)


def setup_inputs(seed: int = 0) -> dict:
    key = jax.random.key(seed)
    ks = iter(jax.random.split(key, 40))
    nrm = lambda shape, s: jax.random.normal(next(ks), shape, jnp.float32) * s
    L = DEPTH
    gain = lambda n: 1.0 + nrm((L, n), 0.02)
    return {
        "x": nrm((BATCH, SEQ, D_MODEL), 1.0),
        "p": nrm((DEPTH, BATCH, SEQ, PLE_DIM), 1.0),
        "g_mix": gain(D_MODEL),
        "w_in": nrm((L, D_MODEL, D_IN), D_MODEL ** -0.5),
        "mu_shift": jax.random.uniform(next(ks), (L, RWKV_COLS), jnp.float32),
        "w0": -0.5 + nrm((L, RWKV_WIDTH), 0.5),
        "w_decay_up": nrm((L, DECAY_LORA, RWKV_WIDTH), 0.5 * DECAY_LORA ** -0.5),
        "a0": nrm((L, RWKV_WIDTH), 0.1),
        "w_aaa_up": nrm((L, AAA_LORA, RWKV_WIDTH), 0.5 * AAA_LORA ** -0.5),
        "w_gate_up": nrm((L, GATE_LORA, RWKV_WIDTH), GATE_LORA ** -0.5),
        "k_k": 0.85 + nrm((L, RWKV_WIDTH), 0.05),
        "k_a": 1.0 + nrm((L, RWKV_WIDTH), 0.05),
        "r_k": nrm((L, RWKV_HEADS, RWKV_HEAD_DIM), 0.1),
        "ln_x_w": gain(RWKV_WIDTH),
        "ln_x_b": nrm((L, RWKV_WIDTH), 0.01),
        "pool_w": nrm((L, POOL_GROUPS, POOL_GROUP_DIM, POOL_GROUP_DIM), POOL_GROUP_DIM ** -0.5),
        "pool_scale": 0.5 + nrm((L, POOL_WIDTH), 0.05),
        "b_gates": nrm((L, GATE_COLS), 0.01),
        "w_out_a": nrm((L, RWKV_WIDTH, D_MODEL), RWKV_WIDTH ** -0.5),
        "w_out_b": nrm((L, POOL_WIDTH, D_MODEL), POOL_WIDTH ** -0.5),
        "w_o": nrm((L, D_MODEL, D_MODEL), D_MODEL ** -0.5),
        "g_mlp": gain(D_MODEL),
        "w_ff1": nrm((L, D_MODEL, D_FF), D_MODEL ** -0.5),
        "w_ff2": nrm((L, D_FF, D_MODEL), D_FF ** -0.5),
        "g_ple": gain(D_MODEL),
        "w_ple_gate": nrm((L, D_MODEL, D_MODEL), D_MODEL ** -0.5),
        "w_ple_proj": nrm((L, PLE_DIM, D_MODEL), PLE_DIM ** -0.5),
        "g_final": 1.0 + nrm((D_MODEL,), 0.02),
    }


def reference(x, p, g_mix, w_in, mu_shift, w0, w_decay_up, a0, w_aaa_up, w_gate_up,
              k_k, k_a, r_k, ln_x_w, ln_x_b, pool_w, pool_scale, b_gates, w_out_a,
              w_out_b, w_o, g_mlp, w_ff1, w_ff2, g_ple, w_ple_gate, w_ple_proj, g_final):
    B, T, D = x.shape
    for i in range(DEPTH):
        h = rmsnorm(x, g_mix[i])
        z = h @ w_in[i]
        z_rwkv = token_shift(z[..., :RWKV_COLS], mu_shift[i])
        z_pool = z[..., RWKV_COLS:RWKV_COLS + POOL_WIDTH]
        gates = jax.nn.sigmoid(z[..., RWKV_COLS + POOL_WIDTH:] + b_gates[i])
        gates = gates.reshape(B, T, N_BRANCHES, D)
        y_a = rwkv7_mix(z_rwkv, w0[i], w_decay_up[i], a0[i], w_aaa_up[i], w_gate_up[i],
                        k_k[i], k_a[i], r_k[i], ln_x_w[i], ln_x_b[i]) @ w_out_a[i]
        y_b = multiscale_pool_mix(z_pool, pool_w[i], pool_scale[i]) @ w_out_b[i]
        merged = gates[:, :, 0] * y_a + gates[:, :, 1] * y_b
        x = x + merged @ w_o[i]
        h = rmsnorm(x, g_mlp[i])
        x = x + jnp.square(jax.nn.relu(h @ w_ff1[i])) @ w_ff2[i]
        h = rmsnorm(x, g_ple[i])
        x = x + jax.nn.sigmoid(h @ w_ple_gate[i]) * (p[i] @ w_ple_proj[i])
    return rmsnorm(x, g_final)
```

```python
import numpy as np
from contextlib import ExitStack
import concourse.bass as bass
import concourse.mybir as mybir
from concourse.bass_utils import run_bass_kernel_spmd

F32 = mybir.dt.float32
BF16 = mybir.dt.bfloat16
ALU = mybir.AluOpType
AF = mybir.ActivationFunctionType

NDMA_SEMS = 12
DEBUG = False
OVERLAP_NORM = False
STAGE = 99
NBLK = 33
NRING = 4
C_ID, C_BO64, C_BO, C_ML, C_MX, C_RM, C_IC, C_NH, C_EPS, NCST = 0, 128, 256, 384, 512, 1024, 1536, 1600, 1601, 1604
V_MU, V_W0, V_A0, V_KK, V_KA, V_RK, V_LW, V_LB, V_PS, V_BG = 0, 14, 18, 22, 26, 30, 34, 38, 42, 46
V_HW0, V_HA0, V_HBG, V_HKA, V_OKA, NPV = 62, 66, 70, 86, 90, 96
HALF_E = 0.30326532985631671


class TT:
    __slots__ = ("name", "w", "r", "pend", "excl")

    def __init__(self, name):
        self.name = name
        self.w = None
        self.r = {}
        self.pend = False
        self.excl = False


class Buf:
    __slots__ = ("h", "t")

    def __init__(self, h, name):
        self.h = h
        self.t = TT(name)


class Sched:
    ENGS = ("pe", "act", "dve", "pool", "sp")

    def __init__(self, nc, es):
        self.nc = nc
        self.es = es
        self.ops = {e: [] for e in self.ENGS}
        self.seq = {e: 0 for e in self.ENGS}
        self.waited = {e: {} for e in self.ENGS}
        self.pe_pending = []
        self.sems = {}
        for e in ("pe", "act", "dve", "pool"):
            self.sems[e] = es.enter_context(nc.semaphore("s_" + e))
        self.dma_cnt = [0] * NDMA_SEMS
        for i in range(NDMA_SEMS):
            self.sems["d%d" % i] = es.enter_context(nc.semaphore("s_d%d" % i))
        self.dma_rr = 0
        self.btoks = []

    def sb(self, name, shape, dt):
        return self.es.enter_context(self.nc.sbuf_tensor("sb_" + name, shape, dt))

    def ps(self, name, shape, dt):
        return self.es.enter_context(self.nc.psum_tensor("ps_" + name, shape, dt))

    def _need(self, e, deps):
        out = {}
        for (k, v) in deps:
            if v > out.get(k, 0):
                out[k] = v
        res = []
        for k, v in out.items():
            if self.waited[e].get(k, 0) >= v:
                continue
            self.waited[e][k] = v
            res.append((k, v))
        return res

    def barrier(self):
        assert not self.pe_pending
        toks = [(e, self.seq[e]) for e in ("pe", "act", "dve", "pool") if self.seq[e] > 0]
        if DEBUG:
            for j in range(NDMA_SEMS):
                if self.dma_cnt[j] > 0:
                    toks.append(("d%d" % j, 16 * self.dma_cnt[j]))
        self.btoks = toks

    def op(self, e, fn, reads=(), writes=(), signal=True):
        if e != "pe":
            ex = [t for t in reads if t.excl]
            if ex:
                reads = [t for t in reads if not t.excl]
                writes = list(writes) + [t for t in ex if t not in writes]
        deps = list(self.btoks)
        for t in reads:
            assert not t.pend or e == "pe", "read of pending PE tile %s" % t.name
            if t.w is not None and not (e == "pe" and t.w[0] == "pe"):
                deps.append(t.w)
        for t in writes:
            assert not t.pend or e == "pe", "write of pending PE tile %s" % t.name
            if t.w is not None and not (e == "pe" and t.w[0] == "pe"):
                deps.append(t.w)
            for k, v in t.r.items():
                if k == e and e == "pe":
                    continue
                deps.append((k, v))
        waits = self._need(e, deps)
        if e == "pe" and not signal:
            self.ops[e].append((fn, waits, None))
            for t in reads:
                self.pe_pending.append((t, "r"))
            for t in writes:
                self.pe_pending.append((t, "w"))
                t.pend = True
            return
        self.seq[e] += 1
        tok = (e, self.seq[e])
        self.ops[e].append((fn, waits, (e, 1)))
        upd = [(t, "r") for t in reads] + [(t, "w") for t in writes]
        if e == "pe":
            upd = self.pe_pending + upd
            self.pe_pending = []
        for t, m in upd:
            if m == "r":
                t.r[tok[0]] = tok[1]
        for t, m in upd:
            if m == "w":
                t.w = tok
                t.r = {}
                t.pend = False

    def dma(self, q, out, in_, reads=(), writes=(), arena=False):
        j = self.dma_rr
        self.dma_rr = (self.dma_rr + 1) % NDMA_SEMS
        key = "d%d" % j
        deps = [] if (q == "sp" and not arena) else list(self.btoks)
        if self.dma_cnt[j] > 0:
            deps.append((key, 16 * self.dma_cnt[j]))
        for t in reads:
            assert not t.pend
            if t.w is not None:
                deps.append(t.w)
        for t in writes:
            assert not t.pend
            if t.w is not None:
                deps.append(t.w)
            for k, v in t.r.items():
                deps.append((k, v))
        waits = self._need(q, deps)
        self.dma_cnt[j] += 1
        tok = (key, 16 * self.dma_cnt[j])

        def fn(eng, out=out, in_=in_):
            return eng.dma_start(out=out, in_=in_)
        self.ops[q].append((fn, waits, (key, 16)))
        for t in reads:
            t.r[key] = tok[1]
        for t in writes:
            t.w = tok
            t.r = {}
        return tok

    def final_wait(self, q, toks):
        waits = self._need(q, toks)
        self.ops[q].append((None, waits, None))

    def emit(self):
        nc = self.nc
        sems = self.sems
        ops = self.ops
        assert not self.pe_pending
        with nc.Block() as block:
            def run(eng, lst):
                for fn, waits, inc in lst:
                    for k, v in waits:
                        eng.wait_ge(sems[k], v)
                    if fn is None:
                        continue
                    ins = fn(eng)
                    if inc is not None:
                        ins.then_inc(sems[inc[0]], inc[1])

            @block.tensor
            def _(eng):
                run(eng, ops["pe"])

            @block.scalar
            def _(eng):
                run(eng, ops["act"])

            @block.vector
            def _(eng):
                run(eng, ops["dve"])

            @block.gpsimd
            def _(eng):
                run(eng, ops["pool"])

            @block.sync
            def _(eng):
                run(eng, ops["sp"])


def build(nb, nt):
    nc = bass.Bass("TRN2", target_bir_lowering=False)
    T = nt * 512
    x_d = nc.dram_tensor("x", [nb, T, 1024], F32, kind="ExternalInput").ap()
    p_d = nc.dram_tensor("p", [nb, T, 256], F32, kind="ExternalInput").ap()
    wh_d = nc.dram_tensor("wh", [NBLK, 128, 4096], F32, kind="ExternalInput").ap()
    pv_d = nc.dram_tensor("pv", [128, 62], F32, kind="ExternalInput").ap()
    gv_d = nc.dram_tensor("gv", [4, 1024], F32, kind="ExternalInput").ap()
    cst_d = nc.dram_tensor("cst", [128, NCST], F32, kind="ExternalInput").ap()
    out_d = nc.dram_tensor("out", [nb, T, 1024], F32, kind="ExternalOutput").ap()
    ws_d = nc.dram_tensor("ws", [NBLK, 128, 4096], BF16, kind="Internal").ap()
    dbg_d = nc.dram_tensor("dbg", [8, 128, 4096], F32, kind="ExternalOutput").ap() if DEBUG else None

    with ExitStack() as es:
        S = Sched(nc, es)

        def SB(name, shape, dt):
            return Buf(S.sb(name, shape, dt), name)

        cst = SB("cst", [128, NCST], F32)
        identb = SB("identb", [128, 128], BF16)
        pvt = SB("pvt", [128, NPV], F32)
        gbuf = SB("gbuf", [128, 1024], F32)
        ring = [SB("ring%d" % i, [128, 4096], BF16) for i in range(NRING)]
        smallw = SB("smallw", [128, 2048], BF16)
        wAB = [SB("wAB%d" % i, [128, 4096], BF16) for i in range(2)]
        xts = [SB("xt%d" % i, [128, 4, 1024], F32) for i in range(2)]
        xnT = SB("xnT", [128, 8, 512], BF16)
        xn = [SB("xn%d" % i, [128, 1024], BF16) for i in range(2)]
        zp = SB("zp", [128, 4, 527], F32)
        halo = SB("halo", [128, 16], F32)
        Sst = [SB("Sst%d" % i, [128, 128], BF16) for i in range(4)]
        stat = SB("stat", [128, 16], F32)
        wcs = [SB("wc%d" % i, [128, 8], F32) for i in range(2)]
        ARENA = 23200
        arena = S.sb("arena", [128, ARENA], F32)
        astate = {"off": 0, "n": 0}

        def A_reset():
            astate["off"] = 0

        def AL(shape, dt):
            n = 1
            for s_ in shape[1:]:
                n *= s_
            ncols = n if dt == F32 else (n + 1) // 2
            off = astate["off"]
            assert off + ncols <= ARENA, "arena overflow %d" % (off + ncols)
            astate["off"] = off + ncols
            astate["n"] += 1
            v = arena[:, off:off + ncols]
            if dt != F32:
                v = v.bitcast(BF16)
            if len(shape) > 2:
                names = " ".join("a%d" % i for i in range(len(shape) - 1))
                kw = {"a%d" % i: shape[i + 1] for i in range(len(shape) - 1)}
                v = v.rearrange("p (%s) -> p %s" % (names, names), **kw)
            return Buf(v, "ar%d" % astate["n"])

        class Bank:
            __slots__ = ("h", "hb", "t")

            def __init__(self, i):
                self.h = S.ps("bank%d" % i, [128, 512], F32)
                self.hb = self.h[:].bitcast(BF16)
                self.t = TT("bank%d" % i)
                self.t.excl = True

        banks = [Bank(i) for i in range(8)]
        pools = {"pb": [0, 1], "pt": [2, 0, 1], "ptn": [2, 3], "ps": [3, 4, 5, 6], "pr": [7]}
        rot = {"pb": 0, "pt": 0, "ptn": 0, "ps": 0, "pr": 0, "ring": 0, "eng": 0}

        def nBank(pool):
            rot[pool] += 1
            lst = pools[pool]
            return banks[lst[rot[pool] % len(lst)]]

        def nPB():
            return nBank("pb")

        def stream(blk, slot=None):
            if slot is None:
                rot["ring"] += 1
                slot = ring[rot["ring"] % NRING]
            if blk not in cast_done:
                cast_block(blk, slot)
            else:
                S.dma("sp", slot.h[:], ws_d[blk], reads=[wst[blk]], writes=[slot.t])
            return slot

        def tts(bufs):
            return [b.t for b in bufs]

        def TTO(e, out, in0, in1, op, R, W):
            S.op(e, lambda g, out=out, in0=in0, in1=in1, op=op: g.tensor_tensor(out=out, in0=in0, in1=in1, op=op),
                 reads=tts(R), writes=tts(W))

        def TS(e, out, in0, s1, s2, op0, op1, R, W):
            if s2 is None:
                S.op(e, lambda g, out=out, in0=in0, s1=s1, op0=op0: g.tensor_scalar(
                    out=out, in0=in0, scalar1=s1, scalar2=None, op0=op0), reads=tts(R), writes=tts(W))
            else:
                S.op(e, lambda g, out=out, in0=in0, s1=s1, s2=s2, op0=op0, op1=op1: g.tensor_scalar(
                    out=out, in0=in0, scalar1=s1, scalar2=s2, op0=op0, op1=op1), reads=tts(R), writes=tts(W))

        def STT(e, out, in0, sc, in1, op0, op1, R, W):
            e = "dve"
            S.op(e, lambda g, out=out, in0=in0, sc=sc, in1=in1, op0=op0, op1=op1: g.scalar_tensor_tensor(
                out=out, in0=in0, scalar=sc, in1=in1, op0=op0, op1=op1), reads=tts(R), writes=tts(W))

        def CP(e, out, in_, R, W):
            if e == "act":
                S.op(e, lambda g, out=out, in_=in_: g.copy(out=out, in_=in_), reads=tts(R), writes=tts(W))
            else:
                S.op(e, lambda g, out=out, in_=in_: g.tensor_copy(out=out, in_=in_), reads=tts(R), writes=tts(W))

        def ACT(out, in_, func, R, W, bias=None, scale=None, accum=None):
            kw = {}
            if bias is not None:
                kw["bias"] = bias
            if scale is not None:
                kw["scale"] = scale
            if accum is not None:
                kw["accum_out"] = accum
            S.op("act", lambda g, out=out, in_=in_, func=func, kw=kw: g.activation(out=out, in_=in_, func=func, **kw),
                 reads=tts(R), writes=tts(W))

        def MSET(e, ap, val, W):
            S.op(e, lambda g, ap=ap, val=val: g.memset(ap, val), writes=tts(W))

        def MM(out, lhsT, rhs, start, stop, R, W, signal):
            S.op("pe", lambda g, out=out, lhsT=lhsT, rhs=rhs, start=start, stop=stop: g.matmul(
                out, lhsT=lhsT, rhs=rhs, start=start, stop=stop), reads=tts(R), writes=tts(W), signal=signal)

        def TR(out, in_, R, W, signal):
            S.op("pe", lambda g, out=out, in_=in_: g.transpose(out=out, in_=in_, identity=identb.h[:]),
                 reads=tts(R) + [identb.t], writes=tts(W), signal=signal)

        def POW(buf_ap, Rw):
            n = buf_ap.shape[1]
            TTO("pool", buf_ap, buf_ap, cst.h[:, C_NH:C_NH + 1].to_broadcast([buf_ap.shape[0], n]), ALU.pow,
                [Rw, cst], [Rw])

        def ev():
            rot["eng"] += 1
            return "dve" if rot["eng"] % 2 else "pool"

        def pvc(col):
            return pvt.h[:, col:col + 1]

        S.dma("sp", cst.h[:], cst_d, writes=[cst.t])
        S.dma("sp", pvt.h[:, 0:62], pv_d, writes=[pvt.t])
        CP("dve", identb.h[:], cst.h[:, C_ID:C_ID + 128], [cst], [identb])
        TS("dve", pvt.h[:, V_HW0:V_HW0 + 4], pvt.h[:, V_W0:V_W0 + 4], 0.5, None, ALU.mult, None, [pvt], [pvt])
        TS("dve", pvt.h[:, V_HA0:V_HA0 + 4], pvt.h[:, V_A0:V_A0 + 4], 0.5, None, ALU.mult, None, [pvt], [pvt])
        TS("dve", pvt.h[:, V_HBG:V_HBG + 16], pvt.h[:, V_BG:V_BG + 16], 0.5, None, ALU.mult, None, [pvt], [pvt])
        TS("dve", pvt.h[:, V_HKA:V_HKA + 4], pvt.h[:, V_KA:V_KA + 4], 0.5, None, ALU.mult, None, [pvt], [pvt])
        TS("dve", pvt.h[:, V_OKA:V_OKA + 4], pvt.h[:, V_KA:V_KA + 4], -0.5, 1.0, ALU.mult, ALU.add, [pvt], [pvt])

        wst = [TT("ws%d" % i) for i in range(NBLK)]
        stg = {"bufs": [Buf(wAB[i].h[:].bitcast(F32), "stgw%d" % i) for i in range(2)], "k": 0, "arena": False}
        for i in range(2):
            stg["bufs"][i].t = wAB[i].t
        st_ = stg["bufs"][0]
        S.dma("sp", st_.h, wh_d[32, :, 0:2048], writes=[st_.t])
        TS("dve", smallw.h[:, 0:1024], st_.h[:, 0:1024], 1.0, None, ALU.mult, None, [st_], [smallw])
        TS("dve", smallw.h[:, 1024:1536], st_.h[:, 1024:1536], 0.5, None, ALU.mult, None, [st_], [smallw])
        TS("dve", smallw.h[:, 1536:2048], st_.h[:, 1536:2048], 1.0, None, ALU.mult, None, [st_], [smallw])
        cast_done = set()

        def cast_block(blk, slot):
            sc = 0.5 if blk in (11, 12, 31) else 1.0
            for h in range(2):
                st_ = stg["bufs"][stg["k"] % len(stg["bufs"])]
                S.dma("sp", st_.h, wh_d[blk, :, h * 2048:(h + 1) * 2048], writes=[st_.t], arena=stg["arena"])
                e = ("act", "dve")[stg["k"] % 2]
                stg["k"] += 1
                dst = slot.h[:, h * 2048:(h + 1) * 2048]
                if e == "act":
                    ACT(dst, st_.h, AF.Copy, [st_], [slot], scale=sc)
                else:
                    TS(e, dst, st_.h, sc, None, ALU.mult, None, [st_], [slot])
            S.dma("sp", ws_d[blk], slot.h[:], reads=[slot.t], writes=[wst[blk]])
            cast_done.add(blk)

        out_toks = []
        dbgbuf = SB("dbgbuf", [128, 2048], F32) if DEBUG else None

        def DBG(slot, src_ap, src_buf, ncols):
            if not DEBUG:
                return
            for o in range(0, ncols, 2048):
                CP("dve", dbgbuf.h[:, 0:2048], src_ap[:, o:o + 2048], [src_buf], [dbgbuf])
                out_toks.append(S.dma("act", dbg_d[slot, :, o:o + 2048], dbgbuf.h[:, 0:2048], reads=[dbgbuf.t]))

        def norm_gen(gi, xt, dst):
            S.dma("sp", gbuf.h[:], gv_d[gi, :].partition_broadcast(128), writes=[gbuf.t])
            MSET("pool", stat.h[:, 0:4], 0.0, [stat])
            for s in range(4):
                ACT(xn[s % 2].h[:], xt.h[:, s, :], AF.Square, [xt], [xn[s % 2], stat], accum=stat.h[:, s:s + 1])
            TS("pool", stat.h[:, 4:8], stat.h[:, 0:4], 1.0 / 1024, 1e-6, ALU.mult, ALU.add, [stat], [stat])
            POW(stat.h[:, 4:8], stat)
            yield
            for s in range(4):
                xb = xn[s % 2]
                STT("dve", xb.h[:], xt.h[:, s, :], stat.h[:, 4 + s:5 + s], gbuf.h[:], ALU.mult, ALU.mult,
                    [xt, stat, gbuf], [xb])
                pt = nBank("ptn")
                for kc in range(8):
                    TR(pt.hb[:, kc * 128:(kc + 1) * 128], xb.h[:, kc * 128:(kc + 1) * 128], [xb], [pt], kc == 7)
                CP("act", dst.h[:, :, s * 128:(s + 1) * 128], pt.hb.rearrange("p (k t) -> p k t", t=128), [pt], [dst])
                yield

        def norm_T(gi, xt, dst=None):
            for _ in norm_gen(gi, xt, xnT if dst is None else dst):
                pass

        prenorm = {"done": False}

        def tile_body(b, j, idx, nxt):
            first = (j == 0)
            rows = slice(j * 512, (j + 1) * 512)
            pools["pb"] = [0, 1]
            A_reset()
            xt = xts[idx % 2]
            if idx == 0:
                S.dma("act", xt.h[:], x_d[b, rows, :].rearrange("(s p) d -> p s d", p=128), writes=[xt.t])

            def finish():
                out_toks.append(S.dma("act", out_d[b, rows, :].rearrange("(s p) d -> p s d", p=128), xt.h[:], reads=[xt.t]))
                S.barrier()
            if STAGE <= 1:
                return finish()
            if not prenorm["done"]:
                norm_T(0, xt)
            prenorm["done"] = False
            if STAGE <= 2:
                return finish()

            win_slots = {}

            def win_slot(bi):
                if bi not in win_slots:
                    win_slots[bi] = stream(bi)
                return win_slots[bi]

            def zmm(ci):
                slot = win_slot(ci // 4)
                c = ci % 4
                pb = nPB()
                for kc in range(8):
                    MM(pb.h[:], slot.h[:, kc * 512 + c * 128: kc * 512 + (c + 1) * 128], xnT.h[:, kc, :],
                       kc == 0, kc == 7, [slot, xnT], [pb], kc == 7)
                return pb

            rwo = AL([128, 4, 512], BF16)
            markA = astate["off"]
            zb = [AL([128, 513], F32) for _ in range(2)]
            dtmp = AL([128, 512], F32)
            zlo = AL([128, 512], F32)
            txa = AL([128, 512], BF16)
            sgp1 = AL([128, 512], BF16)
            zr = AL([128, 512], F32)
            zk = AL([128, 512], F32)
            zv = AL([128, 512], F32)
            tA = AL([128, 512], F32)
            tB = AL([128, 512], F32)
            tC = AL([128, 512], F32)
            tE = AL([128, 512], F32)
            tG = AL([128, 512], F32)
            tFs = [AL([128, 512], F32) for _ in range(2)]
            pC = AL([128, 512], F32)
            pE = AL([128, 512], F32)
            Yb = AL([128, 512], F32)
            PADS = []
            for q_ in range(2):
                PADS.append(dict(APR=AL([128, 8, 2, 2, 64], BF16), Bp=AL([128, 8, 2, 64], BF16),
                                 Kp=AL([128, 8, 2, 64], BF16), BNp=AL([128, 8, 2, 64], BF16),
                                 KPp=AL([128, 8, 2, 64], BF16), Vp=AL([128, 8, 2, 64], BF16)))
            TM = [AL([128, 512], BF16) for _ in range(4)]
            AX = [AL([128, 512], BF16) for _ in range(4)]
            P0 = [AL([128, 128], BF16) for _ in range(4)]
            TT1 = [AL([128, 128], BF16) for _ in range(4)]
            A2V = [AL([128, 128], BF16) for _ in range(4)]
            G = [[AL([128, 384], BF16) for _ in range(2)] for _ in range(4)]
            TTf = [AL([128, 128], BF16) for _ in range(4)]
            X2 = [AL([128, 256], BF16) for _ in range(4)]
            MT = [AL([128, 128], BF16) for _ in range(4)]
            RT = [AL([128, 128], BF16) for _ in range(4)]

            for pad in PADS:
                for bf_ in pad.values():
                    MSET("dve", bf_.h[:], 0.0, [bf_])

            def shift(ci, m, out):
                pb = zmm(ci)
                z_ = zb[ci % 2]
                CP("act", z_.h[:, 1:513], pb.h[:], [pb], [z_])
                CP("pool", z_.h[:, 0:1], halo.h[:, m:m + 1], [halo], [z_])
                CP("pool", halo.h[:, m:m + 1], z_.h[:, 512:513], [z_], [halo])
                TTO("dve", dtmp.h[:], z_.h[:, 0:512], z_.h[:, 1:513], ALU.subtract, [z_], [dtmp])
                STT("dve", out.h[:], dtmp.h[:], pvc(V_MU + m), z_.h[:, 1:513], ALU.mult, ALU.add, [dtmp, z_, pvt], [out])

            def lora_gen():
                shift(0, 0, zlo)
                ACT(txa.h[0:64, :], zlo.h[0:64, :], AF.Tanh, [zlo], [txa])
                CP("dve", txa.h[64:128, :], zlo.h[64:128, :], [zlo], [txa])
                yield
                shift(1, 1, zlo)
                ACT(pC.h[:], zlo.h[:], AF.Tanh, [zlo], [pC], scale=0.5)
                TS("dve", sgp1.h[:], pC.h[:], 1.0, None, ALU.add, None, [pC], [sgp1])
                yield

            def c3(bufh, rws=slice(0, 128)):
                return bufh[rws, :].rearrange("p (c t) -> p c t", t=64)

            def run_gens(gens):
                alive = list(gens)
                while alive:
                    for g_ in list(alive):
                        try:
                            next(g_)
                        except StopIteration:
                            alive.remove(g_)

            def prep_gen(p):
                q = p % 2
                PL_ = "dve"

                def ev_():
                    return "dve"
                pad = PADS[q]
                APR, Bp, Kp, BNp, KPp, Vp = pad["APR"], pad["Bp"], pad["Kp"], pad["BNp"], pad["KPp"], pad["Vp"]
                tF = tFs[q]
                wc = wcs[q]
                shift(2 + 3 * p, 2 + 3 * p, zr)
                yield
                shift(3 + 3 * p, 3 + 3 * p, zk)
                yield
                shift(4 + 3 * p, 4 + 3 * p, zv)
                yield
                pb = nPB()
                MM(pb.h[:], smallw.h[:, p * 128:(p + 1) * 128], txa.h[:], True, True, [smallw, txa], [pb], True)
                ACT(tA.h[:], pb.h[:], AF.Tanh, [pb, pvt], [tA], bias=pvc(V_HW0 + p), scale=0.5)
                pb2 = nPB()
                MM(pb2.h[:], smallw.h[:, 512 + p * 128:512 + (p + 1) * 128], txa.h[:], True, True, [smallw, txa], [pb2], True)
                ACT(tG.h[:], pb2.h[:], AF.Tanh, [pb2, pvt], [tG], bias=pvc(V_HA0 + p), scale=0.5)
                yield
                TS("dve", tA.h[:], tA.h[:], -HALF_E, -HALF_E, ALU.mult, ALU.add, [tA], [tA])
                S.op("dve", lambda g, tA=tA, tB=tB: g.tensor_tensor_scan(out=tB.h[:], data0=cst.h[:, C_RM:C_RM + 512],
                                                                         data1=tA.h[:], initial=0.0, op0=ALU.mult,
                                                                         op1=ALU.add),
                     reads=tts([cst, tA]), writes=tts([tB]))
                TTO(PL_, tA.h[:], tB.h[:], tA.h[:], ALU.subtract, [tA, tB], [tA])
                ACT(tA.h[:], tA.h[:], AF.Exp, [tA], [tA])
                yield
                TS("dve", tC.h[:], zk.h[:], pvc(V_KK + p), None, ALU.mult, None, [zk, pvt], [tC])
                TTO(PL_, tE.h[:], tC.h[:], tC.h[:], ALU.mult, [tC], [tE])
                pb = nPB()
                MM(pb.h[:], cst.h[:, C_BO:C_BO + 128], tE.h[:], True, True, [cst, tE], [pb], True)
                yield
                TS("dve", tE.h[:], pb.h[:], 1e-24, None, ALU.max, None, [pb], [tE])
                ACT(tE.h[:], tE.h[:], AF.Ln, [tE], [tE])
                ACT(tE.h[:], tE.h[:], AF.Exp, [tE], [tE], scale=-0.5)
                TTO(PL_, tC.h[:], tC.h[:], tE.h[:], ALU.mult, [tC, tE], [tC])
                yield
                for h in range(2):
                    r_ = slice(h * 64, (h + 1) * 64)
                    TTO(ev_(), APR.h[r_, :, 0, h, :], c3(tC.h, r_), c3(tA.h, r_), ALU.mult, [tC, tA], [APR])
                yield
                TS("dve", tE.h[:], tG.h[:], pvc(V_HKA + p), pvc(V_OKA + p), ALU.mult, ALU.add, [tG, pvt], [tE])
                TTO(PL_, tE.h[:], zk.h[:], tE.h[:], ALU.mult, [zk, tE], [tE])
                TS("dve", tA.h[:], tG.h[:], 0.5, 0.5, ALU.mult, ALU.add, [tG], [tA])
                yield
                TTO(PL_, tA.h[:], tC.h[:], tA.h[:], ALU.mult, [tC, tA], [tA])
                ACT(tC.h[:], tB.h[:], AF.Exp, [tB], [tC], scale=-1.0)
                yield
                for h in range(2):
                    r_ = slice(h * 64, (h + 1) * 64)
                    TTO(ev_(), Bp.h[r_, :, h, :], c3(tA.h, r_), c3(tC.h, r_), ALU.mult, [tA, tC], [Bp])
                    TTO(ev_(), Kp.h[r_, :, h, :], c3(tE.h, r_), c3(tC.h, r_), ALU.mult, [tE, tC], [Kp])
                    yield
                ACT(tC.h[:], tB.h[:], AF.Exp, [tB], [tC])
                for h in range(2):
                    r_ = slice(h * 64, (h + 1) * 64)
                    TTO(ev_(), APR.h[r_, :, 1, h, :], c3(zr.h, r_), c3(tC.h, r_), ALU.mult, [zr, tC], [APR])
                yield
                Bv = c3(tB.h)
                ACT(wc.h[:].rearrange("p (c o) -> p c o", o=1), Bv[:, :, 63:64], AF.Exp, [tB], [wc])
                TTO("dve", c3(tC.h), Bv[:, :, 63:64].to_broadcast([128, 8, 64]), Bv, ALU.subtract, [tB], [tC])
                ACT(tC.h[:], tC.h[:], AF.Exp, [tC], [tC])
                yield
                for h in range(2):
                    r_ = slice(h * 64, (h + 1) * 64)
                    STT("dve", BNp.h[r_, :, h, :], c3(tA.h, r_), -1.0, c3(tC.h, r_), ALU.mult, ALU.mult, [tA, tC], [BNp])
                    TTO(PL_, KPp.h[r_, :, h, :], c3(tE.h, r_), c3(tC.h, r_), ALU.mult, [tE, tC], [KPp])
                    CP("act", Vp.h[r_, :, h, :], c3(zv.h, r_), [zv], [Vp])
                    yield
                STT("dve", tC.h[:], zr.h[:], pvc(V_RK + p), tE.h[:], ALU.mult, ALU.mult, [zr, tE, pvt], [tC])
                pb = nPB()
                MM(pb.h[:], cst.h[:, C_BO:C_BO + 128], tC.h[:], True, True, [cst, tC], [pb], True)
                TTO("dve", tF.h[:], zv.h[:], pb.h[:], ALU.mult, [zv, pb], [tF])
                yield

            def scan_post_gen(p):
                q = p % 2
                pad = PADS[q]
                APR, Bp, Kp, BNp, KPp, Vp = pad["APR"], pad["Bp"], pad["Kp"], pad["BNp"], pad["KPp"], pad["Vp"]
                tF = tFs[q]
                wc = wcs[q]
                Sb = Sst[p]

                def chunk_local(c):
                    i4 = c % 4
                    tm, ax, p0, tt1, a2v, ttf, x2, mt, rt = (TM[i4], AX[i4], P0[i4], TT1[i4], A2V[i4], TTf[i4],
                                                             X2[i4], MT[i4], RT[i4])
                    a_pad = APR.h[:, c, 0].rearrange("p h t -> p (h t)")
                    r_pad = APR.h[:, c, 1].rearrange("p h t -> p (h t)")
                    ar_pad = APR.h[:, c].rearrange("p q h t -> p (q h t)")
                    b_pad = Bp.h[:, c].rearrange("p h t -> p (h t)")
                    k_pad = Kp.h[:, c].rearrange("p h t -> p (h t)")
                    pt = nBank("pt")
                    half = pt.hb[:, 0:512]
                    TR(half[:, 0:128], a_pad, [APR], [pt], False)
                    TR(half[:, 128:256], BNp.h[:, c].rearrange("p h t -> p (h t)"), [BNp], [pt], False)
                    TR(half[:, 256:384], KPp.h[:, c].rearrange("p h t -> p (h t)"), [KPp], [pt], False)
                    TR(half[:, 384:512], Vp.h[:, c].rearrange("p h t -> p (h t)"), [Vp], [pt], True)
                    CP("act", tm.h[:], half, [pt], [tm])
                    yield
                    X = nBank("ps")
                    MM(X.h[:, 0:256], b_pad, ar_pad, True, True, [Bp, APR], [X], False)
                    MM(X.h[:, 256:512], k_pad, ar_pad, True, True, [Kp, APR], [X], True)
                    TTO("dve", ax.h[:], X.h[:], cst.h[:, C_MX:C_MX + 512], ALU.mult, [X, cst], [ax])
                    Y = nBank("ps")
                    MM(Y.h[:, 0:128], a_pad, b_pad, True, True, [APR, Bp], [Y], True)
                    TTO("dve", p0.h[:], Y.h[:, 0:128], cst.h[:, C_ML:C_ML + 128], ALU.mult, [Y, cst], [p0])
                    yield
                    Y = nBank("ps")
                    MM(Y.h[:, 0:128], ax.h[:, 256:384], tm.h[:, 384:512], True, True, [ax, tm], [Y], False)
                    TTO("pool", tt1.h[:], ax.h[:, 0:128], identb.h[:], ALU.add, [ax, identb], [tt1])
                    Pc, PTc, PcB, PTcB, TTc, TTcB = p0.h[:], ax.h[:, 0:128], p0, ax, tt1.h[:], tt1
                    g_ = G[i4][1]
                    MM(Y.h[:, 128:256], PTc, Pc, True, True, [PcB, PTcB], [Y], False)
                    MM(Y.h[:, 256:384], Pc, PTc, True, True, [PcB, PTcB], [Y], True)
                    CP("act", a2v.h[:], Y.h[:, 0:128], [Y], [a2v])
                    CP("act", g_.h[:, 0:256], Y.h[:, 128:384], [Y], [g_])
                    Pc, PTc, PcB, PTcB = g_.h[:, 0:128], g_.h[:, 128:256], g_, g_
                    yield
                    for r in range(2, 6):
                        Z = nBank("ps")
                        g_ = G[i4][r % 2]
                        MM(Z.h[:, 0:128], PTc, Pc, True, True, [PcB, PTcB], [Z], False)
                        if r < 5:
                            MM(Z.h[:, 128:256], Pc, PTc, True, True, [PcB, PTcB], [Z], False)
                        MM(Z.h[:, 256:384], identb.h[:], TTc, True, False, [identb, TTcB], [Z], False)
                        MM(Z.h[:, 256:384], Pc, TTc, False, True, [PcB, TTcB], [Z], True)
                        CP("act", g_.h[:], Z.h[:, 0:384], [Z], [g_])
                        TTc, TTcB = g_.h[:, 256:384], g_
                        Pc, PTc, PcB, PTcB = g_.h[:, 0:128], g_.h[:, 128:256], g_, g_
                        yield
                    Z = nBank("ps")
                    MM(Z.h[:, 0:128], identb.h[:], TTc, True, False, [identb, TTcB], [Z], False)
                    MM(Z.h[:, 0:128], Pc, TTc, False, True, [PcB, TTcB], [Z], True)
                    CP("act", ttf.h[:], Z.h[:, 0:128], [Z], [ttf])
                    yield
                    Z = nBank("ps")
                    MM(Z.h[:, 0:128], ttf.h[:], tm.h[:, 0:128], True, True, [ttf, tm], [Z], False)
                    MM(Z.h[:, 128:256], ttf.h[:], a2v.h[:], True, True, [ttf, a2v], [Z], True)
                    CP("act", x2.h[:], Z.h[:, 0:256], [Z], [x2])
                    yield
                    Z = nBank("ps")
                    MM(Z.h[:, 0:128], x2.h[:, 0:128], tm.h[:, 128:256], True, True, [x2, tm], [Z], False)
                    MM(Z.h[:, 128:256], x2.h[:, 0:128], ax.h[:, 128:256], True, True, [x2, ax], [Z], True)
                    STT("dve", mt.h[:], cst.h[:, C_ID:C_ID + 128], wc.h[:, c:c + 1], Z.h[:, 0:128], ALU.mult, ALU.add,
                        [cst, wc, Z], [mt])
                    TTO("dve", rt.h[:], r_pad, Z.h[:, 128:256], ALU.add, [APR, Z], [rt])
                    yield

                def chunk_seq(cs):
                    for c in cs:
                        i4 = c % 4
                        tm, ax, x2, mt, rt = TM[i4], AX[i4], X2[i4], MT[i4], RT[i4]
                        PR = nBank("pr")
                        MM(PR.h[:, 128:256], tm.h[:, 256:384], tm.h[:, 384:512], True, False, [tm], [PR], False)
                        MM(PR.h[:, 128:256], tm.h[:, 128:256], x2.h[:, 128:256], False, False, [tm, x2], [PR], False)
                        MM(PR.h[:, 128:256], mt.h[:], Sb.h[:], False, True, [mt, Sb], [PR], False)
                        MM(PR.h[:, 0:128], tm.h[:, 384:512], ax.h[:, 384:512], True, False, [tm, ax], [PR], False)
                        MM(PR.h[:, 0:128], x2.h[:, 128:256], ax.h[:, 128:256], False, False, [x2, ax], [PR], False)
                        MM(PR.h[:, 0:128], Sb.h[:], rt.h[:], False, True, [Sb, rt], [PR], True)
                        CP("dve", Sb.h[:], PR.h[:, 128:256], [PR], [Sb])
                        CP("act", Yb.h[0:64, c * 64:(c + 1) * 64], PR.h[0:64, 0:64], [PR], [Yb])
                        CP("act", Yb.h[64:128, c * 64:(c + 1) * 64], PR.h[64:128, 64:128], [PR], [Yb])
                        yield

                def sweep(alive):
                    for g_ in list(alive):
                        try:
                            next(g_)
                        except StopIteration:
                            alive.remove(g_)

                alive = [chunk_local(c) for c in range(0, 4)]
                while alive:
                    sweep(alive)
                    yield
                seqg = chunk_seq(range(0, 4))
                alive = []
                for k_ in range(4):
                    next(seqg)
                    alive.append(chunk_local(4 + k_))
                    sweep(alive)
                    yield
                while alive:
                    sweep(alive)
                    yield
                for _ in chunk_seq(range(4, 8)):
                    yield
                pb = nPB()
                MM(pb.h[:], cst.h[:, C_BO64:C_BO64 + 128], Yb.h[:], True, True, [cst, Yb], [pb], True)
                TTO("dve", pC.h[:], Yb.h[:], pb.h[:], ALU.subtract, [Yb, pb], [pC])
                TTO("pool", pE.h[:], pC.h[:], pC.h[:], ALU.mult, [pC], [pE])
                yield
                pb = nPB()
                MM(pb.h[:], cst.h[:, C_BO64:C_BO64 + 128], pE.h[:], True, True, [cst, pE], [pb], True)
                ACT(pE.h[:], pb.h[:], AF.Ln, [pb, cst], [pE], bias=cst.h[:, C_EPS:C_EPS + 1])
                ACT(pE.h[:], pE.h[:], AF.Exp, [pE], [pE], scale=-0.5)
                yield
                TTO("dve", pC.h[:], pC.h[:], pE.h[:], ALU.mult, [pC, pE], [pC])
                TS("dve", pC.h[:], pC.h[:], pvc(V_LW + p), pvc(V_LB + p), ALU.mult, ALU.add, [pC, pvt], [pC])
                TTO("pool", pC.h[:], pC.h[:], tF.h[:], ALU.add, [pC, tF], [pC])
                yield
                pb = nPB()
                MM(pb.h[:], smallw.h[:, 1024 + p * 128:1024 + (p + 1) * 128], sgp1.h[:], True, True, [smallw, sgp1], [pb], True)
                TTO("dve", rwo.h[:, p, :], pC.h[:], pb.h[:], ALU.mult, [pC, pb], [rwo])
                yield

            run_gens([lora_gen(), prep_gen(0)])
            for p in range(4):
                gs = [scan_post_gen(p)]
                if p < 3:
                    gs.append(prep_gen(p + 1))
                run_gens(gs)

            if j == 0 and b == 0:
                DBG(0, rwo.h[:].rearrange("p a t -> p (a t)"), rwo, 2048)
            S.barrier()
            astate["off"] = markA
            pools["pb"] = [0, 1, 3, 4, 5, 6]
            if len(cast_done) < NBLK - 1:
                stg["bufs"] = [AL([128, 2048], F32) for _ in range(3)]
                stg["arena"] = True
            pl = AL([128, 4, 512], BF16)
            mx = AL([128, 4, 512], BF16)
            mg = AL([128, 8, 512], BF16)
            u2 = AL([128, 527], F32)
            u4 = AL([128, 527], F32)
            u8 = AL([128, 527], F32)
            u16 = AL([128, 527], F32)
            dtmp = AL([128, 16], F32)
            tA = AL([128, 512], F32)
            tB = AL([128, 512], F32)
            tC = AL([128, 512], F32)
            tE = AL([128, 512], F32)
            mh = [AL([128, 512], F32) for _ in range(8)]
            slA, slB = wAB
            for g in range(4):
                pb = zmm(14 + g)
                CP("pool", zp.h[:, g, 0:15], zp.h[:, g, 512:527], [zp], [zp])
                CP("act", zp.h[:, g, 15:527], pb.h[:], [pb], [zp])
            stream(9, slA)
            stream(10, slB)

            def pool_chain(g):
                win = 2 << g
                Zg = zp.h[:, g, :]
                TTO("dve", u2.h[:, 1:527], Zg[:, 1:527], Zg[:, 0:526], ALU.add, [zp], [u2])
                src = u2
                if g >= 1:
                    TTO("pool", u4.h[:, 3:527], u2.h[:, 3:527], u2.h[:, 1:525], ALU.add, [u2], [u4])
                    src = u4
                if g >= 2:
                    TTO("pool", u8.h[:, 7:527], u4.h[:, 7:527], u4.h[:, 3:523], ALU.add, [u4], [u8])
                    src = u8
                if g >= 3:
                    TTO("pool", u16.h[:, 15:527], u8.h[:, 15:527], u8.h[:, 7:519], ALU.add, [u8], [u16])
                    src = u16
                STT("dve", pl.h[:, g, :], src.h[:, 15:527], 1.0 / win, Zg[:, 15:527], ALU.mult, ALU.subtract,
                    [src, zp], [pl])
                if first:
                    TTO("dve", dtmp.h[:, 0:16], src.h[:, 15:31], cst.h[:, C_IC + g * 16:C_IC + (g + 1) * 16], ALU.mult,
                        [src, cst], [dtmp])
                    TTO("dve", pl.h[:, g, 0:16], dtmp.h[:, 0:16], Zg[:, 15:31], ALU.subtract, [dtmp, zp], [pl])

            tAs = [tA, tB]
            for i in range(8):
                pbA = nPB()
                for kc in range(4):
                    MM(pbA.h[:], slA.h[:, kc * 1024 + i * 128: kc * 1024 + (i + 1) * 128], rwo.h[:, kc, :],
                       kc == 0, kc == 3, [slA, rwo], [pbA], kc == 3)
                pbG = zmm(18 + 2 * i)
                t_ = tAs[i % 2]
                ACT(t_.h[:], pbG.h[:], AF.Tanh, [pbG, pvt], [t_], bias=pvc(V_HBG + i), scale=0.5)
                STT("dve", mh[i].h[:], t_.h[:], 1.0, pbA.h[:], ALU.add, ALU.mult, [t_, pbA], [mh[i]])
                if i < 4:
                    pool_chain(i)
            for g in range(4):
                pb = nPB()
                MM(pb.h[:], smallw.h[:, 1536 + g * 128:1536 + (g + 1) * 128], pl.h[:, g, :], True, True, [smallw, pl], [pb], True)
                TS("dve", mx.h[:, g, :], pb.h[:], pvc(V_PS + g), None, ALU.mult, None, [pb, pvt], [mx])
            if j == 0 and b == 0:
                DBG(1, mx.h[:].rearrange("p a t -> p (a t)"), mx, 2048)
            win_slots.clear()
            for i in range(8):
                pbB = nPB()
                for kc in range(4):
                    MM(pbB.h[:], slB.h[:, kc * 1024 + i * 128: kc * 1024 + (i + 1) * 128], mx.h[:, kc, :],
                       kc == 0, kc == 3, [slB, mx], [pbB], kc == 3)
                pbG = zmm(19 + 2 * i)
                t_ = tAs[i % 2]
                ACT(t_.h[:], pbG.h[:], AF.Tanh, [pbG, pvt], [t_], bias=pvc(V_HBG + 8 + i), scale=0.5)
                t2_ = (tC, tE)[i % 2]
                STT("dve", t2_.h[:], t_.h[:], 1.0, pbB.h[:], ALU.add, ALU.mult, [t_, pbB], [t2_])
                TTO("pool", mg.h[:, i, :], mh[i].h[:], t2_.h[:], ALU.add, [mh[i], t2_], [mg])

            for oh in range(2):
                sl = stream(11 + oh)
                for s in range(4):
                    pb = nPB()
                    for kc in range(8):
                        MM(pb.h[:], mg.h[:, kc, s * 128:(s + 1) * 128], sl.h[:, kc * 512:(kc + 1) * 512],
                           kc == 0, kc == 7, [mg, sl], [pb], kc == 7)
                    xs = xt.h[:, s, oh * 512:(oh + 1) * 512]
                    TTO("dve", xs, xs, pb.h[:], ALU.add, [xt, pb], [xt])

            if STAGE <= 8:
                return finish()
            if j == 0 and b == 0:
                DBG(2, mg.h[:].rearrange("p a t -> p (a t)"), mg, 4096)
                DBG(3, xt.h[:].rearrange("p a t -> p (a t)"), xt, 4096)
            S.barrier()
            pools["pb"] = [0, 1, 4, 5, 6]
            A_reset()
            if len(cast_done) < NBLK - 1:
                stg["bufs"] = [AL([128, 2048], F32) for _ in range(4)]
                stg["arena"] = True
            hid = AL([128, 16, 512], BF16)
            rl = [AL([128, 512], F32) for _ in range(2)]
            ptile = AL([128, 4, 256], F32)
            pbf = AL([128, 4, 256], BF16)
            pT = AL([128, 2, 512], BF16)
            tg = [AL([128, 512], F32) for _ in range(2)]
            t2 = [AL([128, 512], F32) for _ in range(2)]
            xnT2 = AL([128, 8, 512], BF16)
            S.dma("act", ptile.h[:], p_d[b, rows, :].rearrange("(s p) d -> p s d", p=128), writes=[ptile.t])
            if nxt is not None:
                xn_ = xts[(idx + 1) % 2]
                S.dma("act", xn_.h[:], x_d[nxt[0], nxt[1] * 512:(nxt[1] + 1) * 512, :].rearrange("(s p) d -> p s d", p=128),
                      writes=[xn_.t])
            norm_T(1, xt)
            cnt = 0
            for hh in range(2):
                for q in range(4):
                    sl = stream(13 + hh * 8 + q)
                    for c in range(4):
                        pb = nPB()
                        for kc in range(8):
                            MM(pb.h[:], sl.h[:, kc * 512 + c * 128: kc * 512 + (c + 1) * 128], xnT.h[:, kc, :],
                               kc == 0, kc == 7, [sl, xnT], [pb], kc == 7)
                        r_ = rl[cnt % 2]
                        cnt += 1
                        ACT(r_.h[:], pb.h[:], AF.Relu, [pb], [r_])
                        TTO(ev(), hid.h[:, q * 4 + c, :], r_.h[:], r_.h[:], ALU.mult, [r_], [hid])
                for oh in range(2):
                    sls = [stream(17 + hh * 8 + oh * 2 + part) for part in range(2)]
                    for s in range(4):
                        pb = nPB()
                        for part in range(2):
                            for kc in range(8):
                                last = (part == 1 and kc == 7)
                                MM(pb.h[:], hid.h[:, part * 8 + kc, s * 128:(s + 1) * 128],
                                   sls[part].h[:, kc * 512:(kc + 1) * 512],
                                   part == 0 and kc == 0, last, [hid, sls[part]], [pb], last)
                        xs = xt.h[:, s, oh * 512:(oh + 1) * 512]
                        TTO("dve", xs, xs, pb.h[:], ALU.add, [xt, pb], [xt])
            if STAGE <= 9:
                return finish()
            if j == 0 and b == 0:
                DBG(4, xt.h[:].rearrange("p a t -> p (a t)"), xt, 4096)
            norm_T(2, xt, xnT2)
            CP("pool", pbf.h[:], ptile.h[:], [ptile], [pbf])
            for s in range(4):
                pt = nBank("ptn")
                for kc in range(2):
                    TR(pt.hb[:, kc * 128:(kc + 1) * 128], pbf.h[:, s, kc * 128:(kc + 1) * 128], [pbf], [pt], kc == 1)
                CP("act", pT.h[:, :, s * 128:(s + 1) * 128], pt.hb[:, 0:256].rearrange("p (k t) -> p k t", t=128), [pt], [pT])
            slE = stream(31)
            cnt = 0
            ngen = None
            if nxt is not None and OVERLAP_NORM:
                ngen = norm_gen(0, xts[(idx + 1) % 2], xnT)
                prenorm["done"] = True
            for oh in range(2):
                slG = stream(29 + oh)
                for s in range(4):
                    if ngen is not None:
                        try:
                            next(ngen)
                        except StopIteration:
                            ngen = None
                    pbG = nPB()
                    for kc in range(8):
                        MM(pbG.h[:], xnT2.h[:, kc, s * 128:(s + 1) * 128], slG.h[:, kc * 512:(kc + 1) * 512],
                           kc == 0, kc == 7, [xnT2, slG], [pbG], kc == 7)
                    pbE = nPB()
                    for kc in range(2):
                        MM(pbE.h[:], pT.h[:, kc, s * 128:(s + 1) * 128],
                           slE.h[:, kc * 1024 + oh * 512: kc * 1024 + (oh + 1) * 512],
                           kc == 0, kc == 1, [pT, slE], [pbE], kc == 1)
                    tg_ = tg[cnt % 2]
                    t2_ = t2[cnt % 2]
                    cnt += 1
                    ACT(tg_.h[:], pbG.h[:], AF.Tanh, [pbG], [tg_], scale=0.5)
                    STT("dve", t2_.h[:], tg_.h[:], 1.0, pbE.h[:], ALU.add, ALU.mult, [tg_, pbE], [t2_])
                    xs = xt.h[:, s, oh * 512:(oh + 1) * 512]
                    TTO("pool", xs, xs, t2_.h[:], ALU.add, [xt, t2_], [xt])
            if ngen is not None:
                for _ in ngen:
                    pass
            if j == 0 and b == 0:
                DBG(5, xt.h[:].rearrange("p a t -> p (a t)"), xt, 4096)
            S.dma("sp", gbuf.h[:], gv_d[3, :].partition_broadcast(128), writes=[gbuf.t])
            MSET("pool", stat.h[:, 0:4], 0.0, [stat])
            for s in range(4):
                ACT(xn[s % 2].h[:], xt.h[:, s, :], AF.Square, [xt], [xn[s % 2], stat], accum=stat.h[:, s:s + 1])
            TS("pool", stat.h[:, 4:8], stat.h[:, 0:4], 1.0 / 1024, 1e-6, ALU.mult, ALU.add, [stat], [stat])
            POW(stat.h[:, 4:8], stat)
            for s in range(4):
                STT("dve" if s % 2 else "pool", xt.h[:, s, :], xt.h[:, s, :], stat.h[:, 4 + s:5 + s], gbuf.h[:],
                    ALU.mult, ALU.mult, [xt, stat, gbuf], [xt])
            out_toks.append(S.dma("act", out_d[b, rows, :].rearrange("(s p) d -> p s d", p=128), xt.h[:], reads=[xt.t]))
            S.barrier()

        order = [(b, j) for b in range(nb) for j in range(nt)]
        for idx, (b, j) in enumerate(order):
            if j == 0:
                MSET("pool", zp.h[:], 0.0, [zp])
                MSET("pool", halo.h[:], 0.0, [halo])
                for pp in range(4):
                    MSET("pool", Sst[pp].h[:], 0.0, [Sst[pp]])
            tile_body(b, j, idx, order[idx + 1] if idx + 1 < len(order) else None)
        S.final_wait("act", out_toks)
        S.emit()
    return nc


def _fm_blocks(W):
    K, N = W.shape
    nk = K // 128
    out = []
    for g in range(N // 512):
        blk = W[:, g * 512:(g + 1) * 512].reshape(nk, 128, 512).transpose(1, 0, 2).reshape(128, nk * 512)
        out.append(blk)
    return out


def _prep(inp):
    f = lambda a: np.asarray(a, dtype=np.float32)
    w_in = f(inp["w_in"])[0]
    cols = []
    cols.append(np.arange(1536, 1664))
    cols.append(np.arange(1664, 1792))
    for p in range(4):
        for base in (0, 512, 1024):
            cols.append(np.arange(base + p * 128, base + (p + 1) * 128))
    for g in range(4):
        cols.append(np.arange(1792 + g * 128, 1792 + (g + 1) * 128))
    for i in range(8):
        cols.append(np.arange(2304 + i * 128, 2304 + (i + 1) * 128))
        cols.append(np.arange(2304 + (8 + i) * 128, 2304 + (9 + i) * 128))
    cols = np.concatenate(cols)
    assert cols.shape[0] == 4352
    w_in_p = np.concatenate([w_in[:, cols], np.zeros((1024, 256), np.float32)], axis=1)
    blocks = []
    blocks += _fm_blocks(w_in_p)
    for nm in ("w_out_a", "w_out_b"):
        W = f(inp[nm])[0]
        blocks.append(W.reshape(4, 128, 1024).transpose(1, 0, 2).reshape(128, 4096))
    blocks += _fm_blocks(f(inp["w_o"])[0])
    w1 = f(inp["w_ff1"])[0]
    w2 = f(inp["w_ff2"])[0]
    for hh in range(2):
        blocks += _fm_blocks(w1[:, hh * 2048:(hh + 1) * 2048])
        for oh in range(2):
            for part in range(2):
                r0 = hh * 2048 + part * 1024
                sub = w2[r0:r0 + 1024, oh * 512:(oh + 1) * 512]
                blocks.append(sub.reshape(8, 128, 512).transpose(1, 0, 2).reshape(128, 4096))
    blocks += _fm_blocks(f(inp["w_ple_gate"])[0])
    wpe = f(inp["w_ple_proj"])[0]
    b31 = np.zeros((128, 4096), np.float32)
    b31[:, 0:2048] = wpe.reshape(2, 128, 1024).transpose(1, 0, 2).reshape(128, 2048)
    blocks.append(b31)
    b32 = np.zeros((128, 4096), np.float32)
    b32[0:64, 0:512] = f(inp["w_decay_up"])[0]
    b32[64:128, 512:1024] = f(inp["w_aaa_up"])[0]
    b32[:, 1024:1536] = f(inp["w_gate_up"])[0]
    b32[:, 1536:2048] = f(inp["pool_w"])[0].transpose(1, 0, 2).reshape(128, 512)
    blocks.append(b32)
    assert len(blocks) == NBLK
    wh = np.ascontiguousarray(np.stack(blocks, axis=0))

    pv = np.zeros((128, 62), np.float32)
    mu = f(inp["mu_shift"])[0]
    mu_cols = [mu[1536:1664], mu[1664:1792]]
    for p in range(4):
        for base in (0, 512, 1024):
            mu_cols.append(mu[base + p * 128: base + (p + 1) * 128])
    pv[:, V_MU:V_MU + 14] = np.stack(mu_cols, axis=1)
    for nm, col in (("w0", V_W0), ("a0", V_A0), ("k_k", V_KK), ("k_a", V_KA), ("ln_x_w", V_LW), ("ln_x_b", V_LB),
                    ("pool_scale", V_PS)):
        pv[:, col:col + 4] = f(inp[nm])[0].reshape(4, 128).T
    pv[:, V_RK:V_RK + 4] = f(inp["r_k"])[0].reshape(4, 128).T
    pv[:, V_BG:V_BG + 16] = f(inp["b_gates"])[0].reshape(16, 128).T
    gv = np.stack([f(inp["g_mix"])[0], f(inp["g_mlp"])[0], f(inp["g_ple"])[0], f(inp["g_final"])], axis=0)

    cst = np.zeros((128, NCST), np.float32)
    idx = np.arange(128)
    hb = idx // 64
    tt_ = idx % 64
    same = (hb[:, None] == hb[None, :]).astype(np.float32)
    cst[:, C_ID:C_ID + 128] = np.eye(128, dtype=np.float32)
    cst[:, C_BO64:C_BO64 + 128] = same / 64.0
    cst[:, C_BO:C_BO + 128] = same
    cst[:, C_ML:C_ML + 128] = -same * (tt_[None, :] < tt_[:, None])
    lt = same * (tt_[:, None] < tt_[None, :])
    le = same * (tt_[:, None] <= tt_[None, :])
    cst[:, C_MX:C_MX + 128] = -lt
    cst[:, C_MX + 128:C_MX + 256] = -le
    cst[:, C_MX + 256:C_MX + 384] = lt
    cst[:, C_MX + 384:C_MX + 512] = le
    rm = np.ones(512, np.float32)
    rm[::64] = 0.0
    cst[:, C_RM:C_RM + 512] = rm[None, :]
    for g in range(4):
        win = 2 << g
        cst[:, C_IC + g * 16:C_IC + (g + 1) * 16] = (1.0 / np.minimum(np.arange(16) + 1, win))[None, :]
    cst[:, C_NH] = -0.5
    cst[:, C_EPS] = 64e-5
    return wh, pv, gv, cst


_CACHE = {}


def kernel(**inputs):
    x = np.asarray(inputs["x"], dtype=np.float32)
    p = np.asarray(inputs["p"], dtype=np.float32)[0]
    B, T, D = x.shape
    ncores = 8
    nb = B // ncores
    nt = T // 512
    wh, pv, gv, cst = _prep(inputs)
    key = (nb, nt)
    if key not in _CACHE:
        _CACHE[key] = build(nb, nt)
    nc = _CACHE[key]
    in_maps = []
    for c in range(ncores):
        in_maps.append({"x": np.ascontiguousarray(x[c * nb:(c + 1) * nb]),
                        "p": np.ascontiguousarray(p[c * nb:(c + 1) * nb]),
                        "wh": wh, "pv": pv, "gv": gv, "cst": cst})
    res = run_bass_kernel_spmd(nc, in_maps, core_ids=list(range(ncores)))
    out = np.concatenate([np.asarray(r["out"]) for r in res.results], axis=0)
    return out.astype(np.float32)
```

```python
import numpy as np
from contextlib import ExitStack
import concourse.bass as bass
import concourse.mybir as mybir
from concourse.bass_utils import run_bass_kernel_spmd

F32 = mybir.dt.float32
BF16 = mybir.dt.bfloat16
ALU = mybir.AluOpType
AF = mybir.ActivationFunctionType

NDMA_SEMS = 12
DEBUG = False
OVERLAP_NORM = False
STAGE = 99
NBLK = 33
NRING = 4
C_ID, C_BO64, C_BO, C_ML, C_MX, C_RM, C_IC, C_NH, C_EPS, NCST = 0, 128, 256, 384, 512, 1024, 1536, 1600, 1601, 1604
V_MU, V_W0, V_A0, V_KK, V_KA, V_RK, V_LW, V_LB, V_PS, V_BG = 0, 14, 18, 22, 26, 30, 34, 38, 42, 46
V_HW0, V_HA0, V_HBG, V_HKA, V_OKA, NPV = 62, 66, 70, 86, 90, 96
HALF_E = 0.30326532985631671


class TT:
    __slots__ = ("name", "w", "r", "pend", "excl")

    def __init__(self, name):
        self.name = name
        self.w = None
        self.r = {}
        self.pend = False
        self.excl = False


class Buf:
    __slots__ = ("h", "t")

    def __init__(self, h, name):
        self.h = h
        self.t = TT(name)


class Sched:
    ENGS = ("pe", "act", "dve", "pool", "sp")

    def __init__(self, nc, es):
        self.nc = nc
        self.es = es
        self.ops = {e: [] for e in self.ENGS}
        self.seq = {e: 0 for e in self.ENGS}
        self.waited = {e: {} for e in self.ENGS}
        self.pe_pending = []
        self.sems = {}
        for e in ("pe", "act", "dve", "pool"):
            self.sems[e] = es.enter_context(nc.semaphore("s_" + e))
        self.dma_cnt = [0] * NDMA_SEMS
        for i in range(NDMA_SEMS):
            self.sems["d%d" % i] = es.enter_context(nc.semaphore("s_d%d" % i))
        self.dma_rr = 0
        self.btoks = []

    def sb(self, name, shape, dt):
        return self.es.enter_context(self.nc.sbuf_tensor("sb_" + name, shape, dt))

    def ps(self, name, shape, dt):
        return self.es.enter_context(self.nc.psum_tensor("ps_" + name, shape, dt))

    def _need(self, e, deps):
        out = {}
        for (k, v) in deps:
            if v > out.get(k, 0):
                out[k] = v
        res = []
        for k, v in out.items():
            if self.waited[e].get(k, 0) >= v:
                continue
            self.waited[e][k] = v
            res.append((k, v))
        return res

    def barrier(self):
        assert not self.pe_pending
        toks = [(e, self.seq[e]) for e in ("pe", "act", "dve", "pool") if self.seq[e] > 0]
        if DEBUG:
            for j in range(NDMA_SEMS):
                if self.dma_cnt[j] > 0:
                    toks.append(("d%d" % j, 16 * self.dma_cnt[j]))
        self.btoks = toks

    def op(self, e, fn, reads=(), writes=(), signal=True):
        if e != "pe":
            ex = [t for t in reads if t.excl]
            if ex:
                reads = [t for t in reads if not t.excl]
                writes = list(writes) + [t for t in ex if t not in writes]
        deps = list(self.btoks)
        for t in reads:
            assert not t.pend or e == "pe", "read of pending PE tile %s" % t.name
            if t.w is not None and not (e == "pe" and t.w[0] == "pe"):
                deps.append(t.w)
        for t in writes:
            assert not t.pend or e == "pe", "write of pending PE tile %s" % t.name
            if t.w is not None and not (e == "pe" and t.w[0] == "pe"):
                deps.append(t.w)
            for k, v in t.r.items():
                if k == e and e == "pe":
                    continue
                deps.append((k, v))
        waits = self._need(e, deps)
        if e == "pe" and not signal:
            self.ops[e].append((fn, waits, None))
            for t in reads:
                self.pe_pending.append((t, "r"))
            for t in writes:
                self.pe_pending.append((t, "w"))
                t.pend = True
            return
        self.seq[e] += 1
        tok = (e, self.seq[e])
        self.ops[e].append((fn, waits, (e, 1)))
        upd = [(t, "r") for t in reads] + [(t, "w") for t in writes]
        if e == "pe":
            upd = self.pe_pending + upd
            self.pe_pending = []
        for t, m in upd:
            if m == "r":
                t.r[tok[0]] = tok[1]
        for t, m in upd:
            if m == "w":
                t.w = tok
                t.r = {}
                t.pend = False

    def dma(self, q, out, in_, reads=(), writes=(), arena=False):
        j = self.dma_rr
        self.dma_rr = (self.dma_rr + 1) % NDMA_SEMS
        key = "d%d" % j
        deps = [] if (q == "sp" and not arena) else list(self.btoks)
        if self.dma_cnt[j] > 0:
            deps.append((key, 16 * self.dma_cnt[j]))
        for t in reads:
            assert not t.pend
            if t.w is not None:
                deps.append(t.w)
        for t in writes:
            assert not t.pend
            if t.w is not None:
                deps.append(t.w)
            for k, v in t.r.items():
                deps.append((k, v))
        waits = self._need(q, deps)
        self.dma_cnt[j] += 1
        tok = (key, 16 * self.dma_cnt[j])

        def fn(eng, out=out, in_=in_):
            return eng.dma_start(out=out, in_=in_)
        self.ops[q].append((fn, waits, (key, 16)))
        for t in reads:
            t.r[key] = tok[1]
        for t in writes:
            t.w = tok
            t.r = {}
        return tok

    def final_wait(self, q, toks):
        waits = self._need(q, toks)
        self.ops[q].append((None, waits, None))

    def emit(self):
        nc = self.nc
        sems = self.sems
        ops = self.ops
        assert not self.pe_pending
        with nc.Block() as block:
            def run(eng, lst):
                for fn, waits, inc in lst:
                    for k, v in waits:
                        eng.wait_ge(sems[k], v)
                    if fn is None:
                        continue
                    ins = fn(eng)
                    if inc is not None:
                        ins.then_inc(sems[inc[0]], inc[1])

            @block.tensor
            def _(eng):
                run(eng, ops["pe"])

            @block.scalar
            def _(eng):
                run(eng, ops["act"])

            @block.vector
            def _(eng):
                run(eng, ops["dve"])

            @block.gpsimd
            def _(eng):
                run(eng, ops["pool"])

            @block.sync
            def _(eng):
                run(eng, ops["sp"])


def build(nb, nt):
    nc = bass.Bass("TRN2", target_bir_lowering=False)
    T = nt * 512
    x_d = nc.dram_tensor("x", [nb, T, 1024], F32, kind="ExternalInput").ap()
    p_d = nc.dram_tensor("p", [nb, T, 256], F32, kind="ExternalInput").ap()
    wh_d = nc.dram_tensor("wh", [NBLK, 128, 4096], F32, kind="ExternalInput").ap()
    pv_d = nc.dram_tensor("pv", [128, 62], F32, kind="ExternalInput").ap()
    gv_d = nc.dram_tensor("gv", [4, 1024], F32, kind="ExternalInput").ap()
    cst_d = nc.dram_tensor("cst", [128, NCST], F32, kind="ExternalInput").ap()
    out_d = nc.dram_tensor("out", [nb, T, 1024], F32, kind="ExternalOutput").ap()
    ws_d = nc.dram_tensor("ws", [NBLK, 128, 4096], BF16, kind="Internal").ap()
    dbg_d = nc.dram_tensor("dbg", [8, 128, 4096], F32, kind="ExternalOutput").ap() if DEBUG else None

    with ExitStack() as es:
        S = Sched(nc, es)

        def SB(name, shape, dt):
            return Buf(S.sb(name, shape, dt), name)

        cst = SB("cst", [128, NCST], F32)
        identb = SB("identb", [128, 128], BF16)
        pvt = SB("pvt", [128, NPV], F32)
        gbuf = SB("gbuf", [128, 1024], F32)
        ring = [SB("ring%d" % i, [128, 4096], BF16) for i in range(NRING)]
        smallw = SB("smallw", [128, 2048], BF16)
        wAB = [SB("wAB%d" % i, [128, 4096], BF16) for i in range(2)]
        xts = [SB("xt%d" % i, [128, 4, 1024], F32) for i in range(2)]
        xnT = SB("xnT", [128, 8, 512], BF16)
        xn = [SB("xn%d" % i, [128, 1024], BF16) for i in range(2)]
        zp = SB("zp", [128, 4, 527], F32)
        halo = SB("halo", [128, 16], F32)
        Sst = [SB("Sst%d" % i, [128, 128], BF16) for i in range(4)]
        stat = SB("stat", [128, 16], F32)
        wcs = [SB("wc%d" % i, [128, 8], F32) for i in range(2)]
        ARENA = 23200
        arena = S.sb("arena", [128, ARENA], F32)
        astate = {"off": 0, "n": 0}

        def A_reset():
            astate["off"] = 0

        def AL(shape, dt):
            n = 1
            for s_ in shape[1:]:
                n *= s_
            ncols = n if dt == F32 else (n + 1) // 2
            off = astate["off"]
            assert off + ncols <= ARENA, "arena overflow %d" % (off + ncols)
            astate["off"] = off + ncols
            astate["n"] += 1
            v = arena[:, off:off + ncols]
            if dt != F32:
                v = v.bitcast(BF16)
            if len(shape) > 2:
                names = " ".join("a%d" % i for i in range(len(shape) - 1))
                kw = {"a%d" % i: shape[i + 1] for i in range(len(shape) - 1)}
                v = v.rearrange("p (%s) -> p %s" % (names, names), **kw)
            return Buf(v, "ar%d" % astate["n"])

        class Bank:
            __slots__ = ("h", "hb", "t")

            def __init__(self, i):
                self.h = S.ps("bank%d" % i, [128, 512], F32)
                self.hb = self.h[:].bitcast(BF16)
                self.t = TT("bank%d" % i)
                self.t.excl = True

        banks = [Bank(i) for i in range(8)]
        pools = {"pb": [0, 1], "pt": [2, 0], "ptn": [2, 3], "ps": [3, 4, 5, 6], "pr": [7, 1]}
        rot = {"pb": 0, "pt": 0, "ptn": 0, "ps": 0, "pr": 0, "ring": 0, "eng": 0}

        def nBank(pool):
            rot[pool] += 1
            lst = pools[pool]
            return banks[lst[rot[pool] % len(lst)]]

        def nPB():
            return nBank("pb")

        def stream(blk, slot=None):
            if slot is None:
                rot["ring"] += 1
                slot = ring[rot["ring"] % NRING]
            if blk not in cast_done:
                cast_block(blk, slot)
            else:
                S.dma("sp", slot.h[:], ws_d[blk], reads=[wst[blk]], writes=[slot.t])
            return slot

        def tts(bufs):
            return [b.t for b in bufs]

        def TTO(e, out, in0, in1, op, R, W):
            S.op(e, lambda g, out=out, in0=in0, in1=in1, op=op: g.tensor_tensor(out=out, in0=in0, in1=in1, op=op),
                 reads=tts(R), writes=tts(W))

        def TS(e, out, in0, s1, s2, op0, op1, R, W):
            if s2 is None:
                S.op(e, lambda g, out=out, in0=in0, s1=s1, op0=op0: g.tensor_scalar(
                    out=out, in0=in0, scalar1=s1, scalar2=None, op0=op0), reads=tts(R), writes=tts(W))
            else:
                S.op(e, lambda g, out=out, in0=in0, s1=s1, s2=s2, op0=op0, op1=op1: g.tensor_scalar(
                    out=out, in0=in0, scalar1=s1, scalar2=s2, op0=op0, op1=op1), reads=tts(R), writes=tts(W))

        def STT(e, out, in0, sc, in1, op0, op1, R, W):
            e = "dve"
            S.op(e, lambda g, out=out, in0=in0, sc=sc, in1=in1, op0=op0, op1=op1: g.scalar_tensor_tensor(
                out=out, in0=in0, scalar=sc, in1=in1, op0=op0, op1=op1), reads=tts(R), writes=tts(W))

        def CP(e, out, in_, R, W):
            if e == "act":
                S.op(e, lambda g, out=out, in_=in_: g.copy(out=out, in_=in_), reads=tts(R), writes=tts(W))
            else:
                S.op(e, lambda g, out=out, in_=in_: g.tensor_copy(out=out, in_=in_), reads=tts(R), writes=tts(W))

        def ACT(out, in_, func, R, W, bias=None, scale=None, accum=None):
            kw = {}
            if bias is not None:
                kw["bias"] = bias
            if scale is not None:
                kw["scale"] = scale
            if accum is not None:
                kw["accum_out"] = accum
            S.op("act", lambda g, out=out, in_=in_, func=func, kw=kw: g.activation(out=out, in_=in_, func=func, **kw),
                 reads=tts(R), writes=tts(W))

        def MSET(e, ap, val, W):
            S.op(e, lambda g, ap=ap, val=val: g.memset(ap, val), writes=tts(W))

        def MM(out, lhsT, rhs, start, stop, R, W, signal):
            S.op("pe", lambda g, out=out, lhsT=lhsT, rhs=rhs, start=start, stop=stop: g.matmul(
                out, lhsT=lhsT, rhs=rhs, start=start, stop=stop), reads=tts(R), writes=tts(W), signal=signal)

        def TR(out, in_, R, W, signal):
            S.op("pe", lambda g, out=out, in_=in_: g.transpose(out=out, in_=in_, identity=identb.h[:]),
                 reads=tts(R) + [identb.t], writes=tts(W), signal=signal)

        def POW(buf_ap, Rw):
            n = buf_ap.shape[1]
            TTO("pool", buf_ap, buf_ap, cst.h[:, C_NH:C_NH + 1].to_broadcast([buf_ap.shape[0], n]), ALU.pow,
                [Rw, cst], [Rw])

        def ev():
            rot["eng"] += 1
            return "dve" if rot["eng"] % 2 else "pool"

        def pvc(col):
            return pvt.h[:, col:col + 1]

        S.dma("sp", cst.h[:], cst_d, writes=[cst.t])
        S.dma("sp", pvt.h[:, 0:62], pv_d, writes=[pvt.t])
        CP("dve", identb.h[:], cst.h[:, C_ID:C_ID + 128], [cst], [identb])
        TS("dve", pvt.h[:, V_HW0:V_HW0 + 4], pvt.h[:, V_W0:V_W0 + 4], 0.5, None, ALU.mult, None, [pvt], [pvt])
        TS("dve", pvt.h[:, V_HA0:V_HA0 + 4], pvt.h[:, V_A0:V_A0 + 4], 0.5, None, ALU.mult, None, [pvt], [pvt])
        TS("dve", pvt.h[:, V_HBG:V_HBG + 16], pvt.h[:, V_BG:V_BG + 16], 0.5, None, ALU.mult, None, [pvt], [pvt])
        TS("dve", pvt.h[:, V_HKA:V_HKA + 4], pvt.h[:, V_KA:V_KA + 4], 0.5, None, ALU.mult, None, [pvt], [pvt])
        TS("dve", pvt.h[:, V_OKA:V_OKA + 4], pvt.h[:, V_KA:V_KA + 4], -0.5, 1.0, ALU.mult, ALU.add, [pvt], [pvt])

        wst = [TT("ws%d" % i) for i in range(NBLK)]
        stg = {"bufs": [Buf(wAB[i].h[:].bitcast(F32), "stgw%d" % i) for i in range(2)], "k": 0, "arena": False}
        for i in range(2):
            stg["bufs"][i].t = wAB[i].t
        st_ = stg["bufs"][0]
        S.dma("sp", st_.h, wh_d[32, :, 0:2048], writes=[st_.t])
        TS("dve", smallw.h[:, 0:1024], st_.h[:, 0:1024], 1.0, None, ALU.mult, None, [st_], [smallw])
        TS("dve", smallw.h[:, 1024:1536], st_.h[:, 1024:1536], 0.5, None, ALU.mult, None, [st_], [smallw])
        TS("dve", smallw.h[:, 1536:2048], st_.h[:, 1536:2048], 1.0, None, ALU.mult, None, [st_], [smallw])
        cast_done = set()

        def cast_block(blk, slot):
            sc = 0.5 if blk in (11, 12, 31) else 1.0
            for h in range(2):
                st_ = stg["bufs"][stg["k"] % len(stg["bufs"])]
                S.dma("sp", st_.h, wh_d[blk, :, h * 2048:(h + 1) * 2048], writes=[st_.t], arena=stg["arena"])
                e = ("act", "dve")[stg["k"] % 2]
                stg["k"] += 1
                dst = slot.h[:, h * 2048:(h + 1) * 2048]
                if e == "act":
                    ACT(dst, st_.h, AF.Copy, [st_], [slot], scale=sc)
                else:
                    TS(e, dst, st_.h, sc, None, ALU.mult, None, [st_], [slot])
            S.dma("sp", ws_d[blk], slot.h[:], reads=[slot.t], writes=[wst[blk]])
            cast_done.add(blk)

        out_toks = []
        dbgbuf = SB("dbgbuf", [128, 2048], F32) if DEBUG else None

        def DBG(slot, src_ap, src_buf, ncols):
            if not DEBUG:
                return
            for o in range(0, ncols, 2048):
                CP("dve", dbgbuf.h[:, 0:2048], src_ap[:, o:o + 2048], [src_buf], [dbgbuf])
                out_toks.append(S.dma("act", dbg_d[slot, :, o:o + 2048], dbgbuf.h[:, 0:2048], reads=[dbgbuf.t]))

        def norm_gen(gi, xt, dst):
            S.dma("sp", gbuf.h[:], gv_d[gi, :].partition_broadcast(128), writes=[gbuf.t])
            MSET("pool", stat.h[:, 0:4], 0.0, [stat])
            for s in range(4):
                ACT(xn[s % 2].h[:], xt.h[:, s, :], AF.Square, [xt], [xn[s % 2], stat], accum=stat.h[:, s:s + 1])
            TS("pool", stat.h[:, 4:8], stat.h[:, 0:4], 1.0 / 1024, 1e-6, ALU.mult, ALU.add, [stat], [stat])
            POW(stat.h[:, 4:8], stat)
            yield
            for s in range(4):
                xb = xn[s % 2]
                STT("dve", xb.h[:], xt.h[:, s, :], stat.h[:, 4 + s:5 + s], gbuf.h[:], ALU.mult, ALU.mult,
                    [xt, stat, gbuf], [xb])
                pt = nBank("ptn")
                for kc in range(8):
                    TR(pt.hb[:, kc * 128:(kc + 1) * 128], xb.h[:, kc * 128:(kc + 1) * 128], [xb], [pt], kc == 7)
                CP("act", dst.h[:, :, s * 128:(s + 1) * 128], pt.hb.rearrange("p (k t) -> p k t", t=128), [pt], [dst])
                yield

        def norm_T(gi, xt, dst=None):
            for _ in norm_gen(gi, xt, xnT if dst is None else dst):
                pass

        prenorm = {"done": False}

        def tile_body(b, j, idx, nxt):
            first = (j == 0)
            rows = slice(j * 512, (j + 1) * 512)
            pools["pb"] = [0, 1]
            A_reset()
            xt = xts[idx % 2]
            if idx == 0:
                S.dma("act", xt.h[:], x_d[b, rows, :].rearrange("(s p) d -> p s d", p=128), writes=[xt.t])

            def finish():
                out_toks.append(S.dma("act", out_d[b, rows, :].rearrange("(s p) d -> p s d", p=128), xt.h[:], reads=[xt.t]))
                S.barrier()
            if STAGE <= 1:
                return finish()
            if not prenorm["done"]:
                norm_T(0, xt)
            prenorm["done"] = False
            if STAGE <= 2:
                return finish()

            win_slots = {}

            def win_slot(bi):
                if bi not in win_slots:
                    win_slots[bi] = stream(bi)
                return win_slots[bi]

            def zmm(ci):
                slot = win_slot(ci // 4)
                c = ci % 4
                pb = nPB()
                for kc in range(8):
                    MM(pb.h[:], slot.h[:, kc * 512 + c * 128: kc * 512 + (c + 1) * 128], xnT.h[:, kc, :],
                       kc == 0, kc == 7, [slot, xnT], [pb], kc == 7)
                return pb

            rwo = AL([128, 4, 512], BF16)
            markA = astate["off"]
            zb = [AL([128, 513], F32) for _ in range(2)]
            dtmp = AL([128, 512], F32)
            zlo = AL([128, 512], F32)
            txa = AL([128, 512], BF16)
            sgp1 = AL([128, 512], BF16)
            zr = AL([128, 512], F32)
            zk = AL([128, 512], F32)
            zv = AL([128, 512], F32)
            tA = AL([128, 512], F32)
            tB = AL([128, 512], F32)
            tC = AL([128, 512], F32)
            tE = AL([128, 512], F32)
            tG = AL([128, 512], F32)
            tFs = [AL([128, 512], F32) for _ in range(2)]
            pC = AL([128, 512], F32)
            pE = AL([128, 512], F32)
            Yb = AL([128, 512], F32)
            PADS = []
            for q_ in range(2):
                PADS.append(dict(APR=AL([128, 8, 2, 2, 64], BF16), Bp=AL([128, 8, 2, 64], BF16),
                                 Kp=AL([128, 8, 2, 64], BF16), BNp=AL([128, 8, 2, 64], BF16),
                                 KPp=AL([128, 8, 2, 64], BF16), Vp=AL([128, 8, 2, 64], BF16)))
            TM = [AL([128, 512], BF16) for _ in range(4)]
            AX = [AL([128, 512], BF16) for _ in range(4)]
            P0 = [AL([128, 128], BF16) for _ in range(4)]
            TT1 = [AL([128, 128], BF16) for _ in range(4)]
            A2V = [AL([128, 128], BF16) for _ in range(4)]
            G = [[AL([128, 384], BF16) for _ in range(2)] for _ in range(4)]
            TTf = [AL([128, 128], BF16) for _ in range(4)]
            X2 = [AL([128, 256], BF16) for _ in range(4)]
            MT = [AL([128, 128], BF16) for _ in range(4)]
            RT = [AL([128, 128], BF16) for _ in range(4)]

            for pad in PADS:
                for bf_ in pad.values():
                    MSET("dve", bf_.h[:], 0.0, [bf_])

            def shift(ci, m, out):
                pb = zmm(ci)
                z_ = zb[ci % 2]
                CP("act", z_.h[:, 1:513], pb.h[:], [pb], [z_])
                CP("pool", z_.h[:, 0:1], halo.h[:, m:m + 1], [halo], [z_])
                CP("pool", halo.h[:, m:m + 1], z_.h[:, 512:513], [z_], [halo])
                TTO("dve", dtmp.h[:], z_.h[:, 0:512], z_.h[:, 1:513], ALU.subtract, [z_], [dtmp])
                STT("dve", out.h[:], dtmp.h[:], pvc(V_MU + m), z_.h[:, 1:513], ALU.mult, ALU.add, [dtmp, z_, pvt], [out])

            def lora_gen():
                shift(0, 0, zlo)
                ACT(txa.h[0:64, :], zlo.h[0:64, :], AF.Tanh, [zlo], [txa])
                CP("dve", txa.h[64:128, :], zlo.h[64:128, :], [zlo], [txa])
                yield
                shift(1, 1, zlo)
                ACT(pC.h[:], zlo.h[:], AF.Tanh, [zlo], [pC], scale=0.5)
                TS("dve", sgp1.h[:], pC.h[:], 1.0, None, ALU.add, None, [pC], [sgp1])
                yield

            def c3(bufh, rws=slice(0, 128)):
                return bufh[rws, :].rearrange("p (c t) -> p c t", t=64)

            def run_gens(gens):
                alive = list(gens)
                while alive:
                    for g_ in list(alive):
                        try:
                            next(g_)
                        except StopIteration:
                            alive.remove(g_)

            def prep_gen(p):
                q = p % 2
                PL_ = "dve"

                def ev_():
                    return "dve"
                pad = PADS[q]
                APR, Bp, Kp, BNp, KPp, Vp = pad["APR"], pad["Bp"], pad["Kp"], pad["BNp"], pad["KPp"], pad["Vp"]
                tF = tFs[q]
                wc = wcs[q]
                shift(2 + 3 * p, 2 + 3 * p, zr)
                yield
                shift(3 + 3 * p, 3 + 3 * p, zk)
                yield
                shift(4 + 3 * p, 4 + 3 * p, zv)
                yield
                pb = nPB()
                MM(pb.h[:], smallw.h[:, p * 128:(p + 1) * 128], txa.h[:], True, True, [smallw, txa], [pb], True)
                ACT(tA.h[:], pb.h[:], AF.Tanh, [pb, pvt], [tA], bias=pvc(V_HW0 + p), scale=0.5)
                pb2 = nPB()
                MM(pb2.h[:], smallw.h[:, 512 + p * 128:512 + (p + 1) * 128], txa.h[:], True, True, [smallw, txa], [pb2], True)
                ACT(tG.h[:], pb2.h[:], AF.Tanh, [pb2, pvt], [tG], bias=pvc(V_HA0 + p), scale=0.5)
                yield
                TS("dve", tA.h[:], tA.h[:], -HALF_E, -HALF_E, ALU.mult, ALU.add, [tA], [tA])
                S.op("dve", lambda g, tA=tA, tB=tB: g.tensor_tensor_scan(out=tB.h[:], data0=cst.h[:, C_RM:C_RM + 512],
                                                                         data1=tA.h[:], initial=0.0, op0=ALU.mult,
                                                                         op1=ALU.add),
                     reads=tts([cst, tA]), writes=tts([tB]))
                TTO(PL_, tA.h[:], tB.h[:], tA.h[:], ALU.subtract, [tA, tB], [tA])
                ACT(tA.h[:], tA.h[:], AF.Exp, [tA], [tA])
                yield
                TS("dve", tC.h[:], zk.h[:], pvc(V_KK + p), None, ALU.mult, None, [zk, pvt], [tC])
                TTO(PL_, tE.h[:], tC.h[:], tC.h[:], ALU.mult, [tC], [tE])
                pb = nPB()
                MM(pb.h[:], cst.h[:, C_BO:C_BO + 128], tE.h[:], True, True, [cst, tE], [pb], True)
                yield
                TS("dve", tE.h[:], pb.h[:], 1e-24, None, ALU.max, None, [pb], [tE])
                ACT(tE.h[:], tE.h[:], AF.Ln, [tE], [tE])
                ACT(tE.h[:], tE.h[:], AF.Exp, [tE], [tE], scale=-0.5)
                TTO(PL_, tC.h[:], tC.h[:], tE.h[:], ALU.mult, [tC, tE], [tC])
                yield
                for h in range(2):
                    r_ = slice(h * 64, (h + 1) * 64)
                    TTO(ev_(), APR.h[r_, :, 0, h, :], c3(tC.h, r_), c3(tA.h, r_), ALU.mult, [tC, tA], [APR])
                yield
                TS("dve", tE.h[:], tG.h[:], pvc(V_HKA + p), pvc(V_OKA + p), ALU.mult, ALU.add, [tG, pvt], [tE])
                TTO(PL_, tE.h[:], zk.h[:], tE.h[:], ALU.mult, [zk, tE], [tE])
                TS("dve", tA.h[:], tG.h[:], 0.5, 0.5, ALU.mult, ALU.add, [tG], [tA])
                yield
                TTO(PL_, tA.h[:], tC.h[:], tA.h[:], ALU.mult, [tC, tA], [tA])
                ACT(tC.h[:], tB.h[:], AF.Exp, [tB], [tC], scale=-1.0)
                yield
                for h in range(2):
                    r_ = slice(h * 64, (h + 1) * 64)
                    TTO(ev_(), Bp.h[r_, :, h, :], c3(tA.h, r_), c3(tC.h, r_), ALU.mult, [tA, tC], [Bp])
                    TTO(ev_(), Kp.h[r_, :, h, :], c3(tE.h, r_), c3(tC.h, r_), ALU.mult, [tE, tC], [Kp])
                    yield
                ACT(tC.h[:], tB.h[:], AF.Exp, [tB], [tC])
                for h in range(2):
                    r_ = slice(h * 64, (h + 1) * 64)
                    TTO(ev_(), APR.h[r_, :, 1, h, :], c3(zr.h, r_), c3(tC.h, r_), ALU.mult, [zr, tC], [APR])
                yield
                Bv = c3(tB.h)
                ACT(wc.h[:].rearrange("p (c o) -> p c o", o=1), Bv[:, :, 63:64], AF.Exp, [tB], [wc])
                TTO("dve", c3(tC.h), Bv[:, :, 63:64].to_broadcast([128, 8, 64]), Bv, ALU.subtract, [tB], [tC])
                ACT(tC.h[:], tC.h[:], AF.Exp, [tC], [tC])
                yield
                for h in range(2):
                    r_ = slice(h * 64, (h + 1) * 64)
                    STT("dve", BNp.h[r_, :, h, :], c3(tA.h, r_), -1.0, c3(tC.h, r_), ALU.mult, ALU.mult, [tA, tC], [BNp])
                    TTO(PL_, KPp.h[r_, :, h, :], c3(tE.h, r_), c3(tC.h, r_), ALU.mult, [tE, tC], [KPp])
                    CP("act", Vp.h[r_, :, h, :], c3(zv.h, r_), [zv], [Vp])
                    yield
                STT("dve", tC.h[:], zr.h[:], pvc(V_RK + p), tE.h[:], ALU.mult, ALU.mult, [zr, tE, pvt], [tC])
                pb = nPB()
                MM(pb.h[:], cst.h[:, C_BO:C_BO + 128], tC.h[:], True, True, [cst, tC], [pb], True)
                TTO("dve", tF.h[:], zv.h[:], pb.h[:], ALU.mult, [zv, pb], [tF])
                yield

            def scan_post_gen(p):
                q = p % 2
                pad = PADS[q]
                APR, Bp, Kp, BNp, KPp, Vp = pad["APR"], pad["Bp"], pad["Kp"], pad["BNp"], pad["KPp"], pad["Vp"]
                tF = tFs[q]
                wc = wcs[q]
                Sb = Sst[p]

                def chunk_local(c):
                    i4 = c % 4
                    tm, ax, p0, tt1, a2v, ttf, x2, mt, rt = (TM[i4], AX[i4], P0[i4], TT1[i4], A2V[i4], TTf[i4],
                                                             X2[i4], MT[i4], RT[i4])
                    a_pad = APR.h[:, c, 0].rearrange("p h t -> p (h t)")
                    r_pad = APR.h[:, c, 1].rearrange("p h t -> p (h t)")
                    ar_pad = APR.h[:, c].rearrange("p q h t -> p (q h t)")
                    b_pad = Bp.h[:, c].rearrange("p h t -> p (h t)")
                    k_pad = Kp.h[:, c].rearrange("p h t -> p (h t)")
                    pt = nBank("pt")
                    half = pt.hb[:, 0:512]
                    TR(half[:, 0:128], a_pad, [APR], [pt], False)
                    TR(half[:, 128:256], BNp.h[:, c].rearrange("p h t -> p (h t)"), [BNp], [pt], False)
                    TR(half[:, 256:384], KPp.h[:, c].rearrange("p h t -> p (h t)"), [KPp], [pt], False)
                    TR(half[:, 384:512], Vp.h[:, c].rearrange("p h t -> p (h t)"), [Vp], [pt], True)
                    CP("act", tm.h[:], half, [pt], [tm])
                    yield
                    X = nBank("ps")
                    MM(X.h[:, 0:256], b_pad, ar_pad, True, True, [Bp, APR], [X], False)
                    MM(X.h[:, 256:512], k_pad, ar_pad, True, True, [Kp, APR], [X], True)
                    TTO("dve", ax.h[:], X.h[:], cst.h[:, C_MX:C_MX + 512], ALU.mult, [X, cst], [ax])
                    Y = nBank("ps")
                    MM(Y.h[:, 0:128], a_pad, b_pad, True, True, [APR, Bp], [Y], True)
                    TTO("dve", p0.h[:], Y.h[:, 0:128], cst.h[:, C_ML:C_ML + 128], ALU.mult, [Y, cst], [p0])
                    yield
                    Y = nBank("ps")
                    MM(Y.h[:, 0:128], ax.h[:, 256:384], tm.h[:, 384:512], True, True, [ax, tm], [Y], False)
                    TTO("pool", tt1.h[:], ax.h[:, 0:128], identb.h[:], ALU.add, [ax, identb], [tt1])
                    Pc, PTc, PcB, PTcB, TTc, TTcB = p0.h[:], ax.h[:, 0:128], p0, ax, tt1.h[:], tt1
                    g_ = G[i4][1]
                    MM(Y.h[:, 128:256], PTc, Pc, True, True, [PcB, PTcB], [Y], False)
                    MM(Y.h[:, 256:384], Pc, PTc, True, True, [PcB, PTcB], [Y], True)
                    CP("act", a2v.h[:], Y.h[:, 0:128], [Y], [a2v])
                    CP("act", g_.h[:, 0:256], Y.h[:, 128:384], [Y], [g_])
                    Pc, PTc, PcB, PTcB = g_.h[:, 0:128], g_.h[:, 128:256], g_, g_
                    yield
                    for r in range(2, 6):
                        Z = nBank("ps")
                        g_ = G[i4][r % 2]
                        MM(Z.h[:, 0:128], PTc, Pc, True, True, [PcB, PTcB], [Z], False)
                        if r < 5:
                            MM(Z.h[:, 128:256], Pc, PTc, True, True, [PcB, PTcB], [Z], False)
                        MM(Z.h[:, 256:384], identb.h[:], TTc, True, False, [identb, TTcB], [Z], False)
                        MM(Z.h[:, 256:384], Pc, TTc, False, True, [PcB, TTcB], [Z], True)
                        CP("act", g_.h[:], Z.h[:, 0:384], [Z], [g_])
                        TTc, TTcB = g_.h[:, 256:384], g_
                        Pc, PTc, PcB, PTcB = g_.h[:, 0:128], g_.h[:, 128:256], g_, g_
                        yield
                    Z = nBank("ps")
                    MM(Z.h[:, 0:128], identb.h[:], TTc, True, False, [identb, TTcB], [Z], False)
                    MM(Z.h[:, 0:128], Pc, TTc, False, True, [PcB, TTcB], [Z], True)
                    CP("act", ttf.h[:], Z.h[:, 0:128], [Z], [ttf])
                    yield
                    Z = nBank("ps")
                    MM(Z.h[:, 0:128], ttf.h[:], tm.h[:, 0:128], True, True, [ttf, tm], [Z], False)
                    MM(Z.h[:, 128:256], ttf.h[:], a2v.h[:], True, True, [ttf, a2v], [Z], True)
                    CP("act", x2.h[:], Z.h[:, 0:256], [Z], [x2])
                    yield
                    Z = nBank("ps")
                    MM(Z.h[:, 0:128], x2.h[:, 0:128], tm.h[:, 128:256], True, True, [x2, tm], [Z], False)
                    MM(Z.h[:, 128:256], x2.h[:, 0:128], ax.h[:, 128:256], True, True, [x2, ax], [Z], True)
                    STT("dve", mt.h[:], cst.h[:, C_ID:C_ID + 128], wc.h[:, c:c + 1], Z.h[:, 0:128], ALU.mult, ALU.add,
                        [cst, wc, Z], [mt])
                    TTO("dve", rt.h[:], r_pad, Z.h[:, 128:256], ALU.add, [APR, Z], [rt])
                    yield

                def chunk_seq(cs):
                    for c in cs:
                        i4 = c % 4
                        tm, ax, x2, mt, rt = TM[i4], AX[i4], X2[i4], MT[i4], RT[i4]
                        PR = nBank("pr")
                        MM(PR.h[:, 128:256], tm.h[:, 256:384], tm.h[:, 384:512], True, False, [tm], [PR], False)
                        MM(PR.h[:, 128:256], tm.h[:, 128:256], x2.h[:, 128:256], False, False, [tm, x2], [PR], False)
                        MM(PR.h[:, 128:256], mt.h[:], Sb.h[:], False, True, [mt, Sb], [PR], False)
                        MM(PR.h[:, 0:128], tm.h[:, 384:512], ax.h[:, 384:512], True, False, [tm, ax], [PR], False)
                        MM(PR.h[:, 0:128], x2.h[:, 128:256], ax.h[:, 128:256], False, False, [x2, ax], [PR], False)
                        MM(PR.h[:, 0:128], Sb.h[:], rt.h[:], False, True, [Sb, rt], [PR], True)
                        CP("dve", Sb.h[:], PR.h[:, 128:256], [PR], [Sb])
                        CP("act", Yb.h[0:64, c * 64:(c + 1) * 64], PR.h[0:64, 0:64], [PR], [Yb])
                        CP("act", Yb.h[64:128, c * 64:(c + 1) * 64], PR.h[64:128, 64:128], [PR], [Yb])
                        yield

                def sweep(alive):
                    for g_ in list(alive):
                        try:
                            next(g_)
                        except StopIteration:
                            alive.remove(g_)

                alive = [chunk_local(c) for c in range(0, 4)]
                while alive:
                    sweep(alive)
                    yield
                seqg = chunk_seq(range(0, 4))
                alive = []
                for k_ in range(4):
                    next(seqg)
                    alive.append(chunk_local(4 + k_))
                    sweep(alive)
                    yield
                while alive:
                    sweep(alive)
                    yield
                for _ in chunk_seq(range(4, 8)):
                    yield
                pb = nPB()
                MM(pb.h[:], cst.h[:, C_BO64:C_BO64 + 128], Yb.h[:], True, True, [cst, Yb], [pb], True)
                TTO("dve", pC.h[:], Yb.h[:], pb.h[:], ALU.subtract, [Yb, pb], [pC])
                TTO("pool", pE.h[:], pC.h[:], pC.h[:], ALU.mult, [pC], [pE])
                yield
                pb = nPB()
                MM(pb.h[:], cst.h[:, C_BO64:C_BO64 + 128], pE.h[:], True, True, [cst, pE], [pb], True)
                ACT(pE.h[:], pb.h[:], AF.Ln, [pb, cst], [pE], bias=cst.h[:, C_EPS:C_EPS + 1])
                ACT(pE.h[:], pE.h[:], AF.Exp, [pE], [pE], scale=-0.5)
                yield
                TTO("dve", pC.h[:], pC.h[:], pE.h[:], ALU.mult, [pC, pE], [pC])
                TS("dve", pC.h[:], pC.h[:], pvc(V_LW + p), pvc(V_LB + p), ALU.mult, ALU.add, [pC, pvt], [pC])
                TTO("pool", pC.h[:], pC.h[:], tF.h[:], ALU.add, [pC, tF], [pC])
                yield
                pb = nPB()
                MM(pb.h[:], smallw.h[:, 1024 + p * 128:1024 + (p + 1) * 128], sgp1.h[:], True, True, [smallw, sgp1], [pb], True)
                TTO("dve", rwo.h[:, p, :], pC.h[:], pb.h[:], ALU.mult, [pC, pb], [rwo])
                yield

            run_gens([lora_gen(), prep_gen(0)])
            for p in range(4):
                gs = [scan_post_gen(p)]
                if p < 3:
                    gs.append(prep_gen(p + 1))
                run_gens(gs)

            if j == 0 and b == 0:
                DBG(0, rwo.h[:].rearrange("p a t -> p (a t)"), rwo, 2048)
            S.barrier()
            astate["off"] = markA
            pools["pb"] = [0, 1, 3, 4, 5, 6]
            if len(cast_done) < NBLK - 1:
                stg["bufs"] = [AL([128, 2048], F32) for _ in range(3)]
                stg["arena"] = True
            pl = AL([128, 4, 512], BF16)
            mx = AL([128, 4, 512], BF16)
            mg = AL([128, 8, 512], BF16)
            u2 = AL([128, 527], F32)
            u4 = AL([128, 527], F32)
            u8 = AL([128, 527], F32)
            u16 = AL([128, 527], F32)
            dtmp = AL([128, 16], F32)
            tA = AL([128, 512], F32)
            tB = AL([128, 512], F32)
            tC = AL([128, 512], F32)
            tE = AL([128, 512], F32)
            mh = [AL([128, 512], F32) for _ in range(8)]
            slA, slB = wAB
            for g in range(4):
                pb = zmm(14 + g)
                CP("pool", zp.h[:, g, 0:15], zp.h[:, g, 512:527], [zp], [zp])
                CP("act", zp.h[:, g, 15:527], pb.h[:], [pb], [zp])
            stream(9, slA)
            stream(10, slB)

            def pool_chain(g):
                win = 2 << g
                Zg = zp.h[:, g, :]
                TTO("dve", u2.h[:, 1:527], Zg[:, 1:527], Zg[:, 0:526], ALU.add, [zp], [u2])
                src = u2
                if g >= 1:
                    TTO("pool", u4.h[:, 3:527], u2.h[:, 3:527], u2.h[:, 1:525], ALU.add, [u2], [u4])
                    src = u4
                if g >= 2:
                    TTO("pool", u8.h[:, 7:527], u4.h[:, 7:527], u4.h[:, 3:523], ALU.add, [u4], [u8])
                    src = u8
                if g >= 3:
                    TTO("pool", u16.h[:, 15:527], u8.h[:, 15:527], u8.h[:, 7:519], ALU.add, [u8], [u16])
                    src = u16
                STT("dve", pl.h[:, g, :], src.h[:, 15:527], 1.0 / win, Zg[:, 15:527], ALU.mult, ALU.subtract,
                    [src, zp], [pl])
                if first:
                    TTO("dve", dtmp.h[:, 0:16], src.h[:, 15:31], cst.h[:, C_IC + g * 16:C_IC + (g + 1) * 16], ALU.mult,
                        [src, cst], [dtmp])
                    TTO("dve", pl.h[:, g, 0:16], dtmp.h[:, 0:16], Zg[:, 15:31], ALU.subtract, [dtmp, zp], [pl])

            tAs = [tA, tB]
            for i in range(8):
                pbA = nPB()
                for kc in range(4):
                    MM(pbA.h[:], slA.h[:, kc * 1024 + i * 128: kc * 1024 + (i + 1) * 128], rwo.h[:, kc, :],
                       kc == 0, kc == 3, [slA, rwo], [pbA], kc == 3)
                pbG = zmm(18 + 2 * i)
                t_ = tAs[i % 2]
                ACT(t_.h[:], pbG.h[:], AF.Tanh, [pbG, pvt], [t_], bias=pvc(V_HBG + i), scale=0.5)
                STT("dve", mh[i].h[:], t_.h[:], 1.0, pbA.h[:], ALU.add, ALU.mult, [t_, pbA], [mh[i]])
                if i < 4:
                    pool_chain(i)
            for g in range(4):
                pb = nPB()
                MM(pb.h[:], smallw.h[:, 1536 + g * 128:1536 + (g + 1) * 128], pl.h[:, g, :], True, True, [smallw, pl], [pb], True)
                TS("dve", mx.h[:, g, :], pb.h[:], pvc(V_PS + g), None, ALU.mult, None, [pb, pvt], [mx])
            if j == 0 and b == 0:
                DBG(1, mx.h[:].rearrange("p a t -> p (a t)"), mx, 2048)
            win_slots.clear()
            for i in range(8):
                pbB = nPB()
                for kc in range(4):
                    MM(pbB.h[:], slB.h[:, kc * 1024 + i * 128: kc * 1024 + (i + 1) * 128], mx.h[:, kc, :],
                       kc == 0, kc == 3, [slB, mx], [pbB], kc == 3)
                pbG = zmm(19 + 2 * i)
                t_ = tAs[i % 2]
                ACT(t_.h[:], pbG.h[:], AF.Tanh, [pbG, pvt], [t_], bias=pvc(V_HBG + 8 + i), scale=0.5)
                t2_ = (tC, tE)[i % 2]
                STT("dve", t2_.h[:], t_.h[:], 1.0, pbB.h[:], ALU.add, ALU.mult, [t_, pbB], [t2_])
                TTO("pool", mg.h[:, i, :], mh[i].h[:], t2_.h[:], ALU.add, [mh[i], t2_], [mg])

            for oh in range(2):
                sl = stream(11 + oh)
                for s in range(4):
                    pb = nPB()
                    for kc in range(8):
                        MM(pb.h[:], mg.h[:, kc, s * 128:(s + 1) * 128], sl.h[:, kc * 512:(kc + 1) * 512],
                           kc == 0, kc == 7, [mg, sl], [pb], kc == 7)
                    xs = xt.h[:, s, oh * 512:(oh + 1) * 512]
                    TTO("dve", xs, xs, pb.h[:], ALU.add, [xt, pb], [xt])

            if STAGE <= 8:
                return finish()
            if j == 0 and b == 0:
                DBG(2, mg.h[:].rearrange("p a t -> p (a t)"), mg, 4096)
                DBG(3, xt.h[:].rearrange("p a t -> p (a t)"), xt, 4096)
            S.barrier()
            pools["pb"] = [0, 1, 4, 5, 6]
            A_reset()
            if len(cast_done) < NBLK - 1:
                stg["bufs"] = [AL([128, 2048], F32) for _ in range(4)]
                stg["arena"] = True
            hid = AL([128, 16, 512], BF16)
            rl = [AL([128, 512], F32) for _ in range(2)]
            ptile = AL([128, 4, 256], F32)
            pbf = AL([128, 4, 256], BF16)
            pT = AL([128, 2, 512], BF16)
            tg = [AL([128, 512], F32) for _ in range(2)]
            t2 = [AL([128, 512], F32) for _ in range(2)]
            xnT2 = AL([128, 8, 512], BF16)
            S.dma("act", ptile.h[:], p_d[b, rows, :].rearrange("(s p) d -> p s d", p=128), writes=[ptile.t])
            if nxt is not None:
                xn_ = xts[(idx + 1) % 2]
                S.dma("act", xn_.h[:], x_d[nxt[0], nxt[1] * 512:(nxt[1] + 1) * 512, :].rearrange("(s p) d -> p s d", p=128),
                      writes=[xn_.t])
            norm_T(1, xt)
            cnt = 0
            for hh in range(2):
                for q in range(4):
                    sl = stream(13 + hh * 8 + q)
                    for c in range(4):
                        pb = nPB()
                        for kc in range(8):
                            MM(pb.h[:], sl.h[:, kc * 512 + c * 128: kc * 512 + (c + 1) * 128], xnT.h[:, kc, :],
                               kc == 0, kc == 7, [sl, xnT], [pb], kc == 7)
                        r_ = rl[cnt % 2]
                        cnt += 1
                        ACT(r_.h[:], pb.h[:], AF.Relu, [pb], [r_])
                        TTO(ev(), hid.h[:, q * 4 + c, :], r_.h[:], r_.h[:], ALU.mult, [r_], [hid])
                for oh in range(2):
                    sls = [stream(17 + hh * 8 + oh * 2 + part) for part in range(2)]
                    for s in range(4):
                        pb = nPB()
                        for part in range(2):
                            for kc in range(8):
                                last = (part == 1 and kc == 7)
                                MM(pb.h[:], hid.h[:, part * 8 + kc, s * 128:(s + 1) * 128],
                                   sls[part].h[:, kc * 512:(kc + 1) * 512],
                                   part == 0 and kc == 0, last, [hid, sls[part]], [pb], last)
                        xs = xt.h[:, s, oh * 512:(oh + 1) * 512]
                        TTO("dve", xs, xs, pb.h[:], ALU.add, [xt, pb], [xt])
            if STAGE <= 9:
                return finish()
            if j == 0 and b == 0:
                DBG(4, xt.h[:].rearrange("p a t -> p (a t)"), xt, 4096)
            norm_T(2, xt, xnT2)
            CP("pool", pbf.h[:], ptile.h[:], [ptile], [pbf])
            for s in range(4):
                pt = nBank("ptn")
                for kc in range(2):
                    TR(pt.hb[:, kc * 128:(kc + 1) * 128], pbf.h[:, s, kc * 128:(kc + 1) * 128], [pbf], [pt], kc == 1)
                CP("act", pT.h[:, :, s * 128:(s + 1) * 128], pt.hb[:, 0:256].rearrange("p (k t) -> p k t", t=128), [pt], [pT])
            slE = stream(31)
            cnt = 0
            ngen = None
            if nxt is not None and OVERLAP_NORM:
                ngen = norm_gen(0, xts[(idx + 1) % 2], xnT)
                prenorm["done"] = True
            for oh in range(2):
                slG = stream(29 + oh)
                for s in range(4):
                    if ngen is not None:
                        try:
                            next(ngen)
                        except StopIteration:
                            ngen = None
                    pbG = nPB()
                    for kc in range(8):
                        MM(pbG.h[:], xnT2.h[:, kc, s * 128:(s + 1) * 128], slG.h[:, kc * 512:(kc + 1) * 512],
                           kc == 0, kc == 7, [xnT2, slG], [pbG], kc == 7)
                    pbE = nPB()
                    for kc in range(2):
                        MM(pbE.h[:], pT.h[:, kc, s * 128:(s + 1) * 128],
                           slE.h[:, kc * 1024 + oh * 512: kc * 1024 + (oh + 1) * 512],
                           kc == 0, kc == 1, [pT, slE], [pbE], kc == 1)
                    tg_ = tg[cnt % 2]
                    t2_ = t2[cnt % 2]
                    cnt += 1
                    ACT(tg_.h[:], pbG.h[:], AF.Tanh, [pbG], [tg_], scale=0.5)
                    STT("dve", t2_.h[:], tg_.h[:], 1.0, pbE.h[:], ALU.add, ALU.mult, [tg_, pbE], [t2_])
                    xs = xt.h[:, s, oh * 512:(oh + 1) * 512]
                    TTO("pool", xs, xs, t2_.h[:], ALU.add, [xt, t2_], [xt])
            if ngen is not None:
                for _ in ngen:
                    pass
            if j == 0 and b == 0:
                DBG(5, xt.h[:].rearrange("p a t -> p (a t)"), xt, 4096)
            S.dma("sp", gbuf.h[:], gv_d[3, :].partition_broadcast(128), writes=[gbuf.t])
            MSET("pool", stat.h[:, 0:4], 0.0, [stat])
            for s in range(4):
                ACT(xn[s % 2].h[:], xt.h[:, s, :], AF.Square, [xt], [xn[s % 2], stat], accum=stat.h[:, s:s + 1])
            TS("pool", stat.h[:, 4:8], stat.h[:, 0:4], 1.0 / 1024, 1e-6, ALU.mult, ALU.add, [stat], [stat])
            POW(stat.h[:, 4:8], stat)
            for s in range(4):
                STT("dve" if s % 2 else "pool", xt.h[:, s, :], xt.h[:, s, :], stat.h[:, 4 + s:5 + s], gbuf.h[:],
                    ALU.mult, ALU.mult, [xt, stat, gbuf], [xt])
            out_toks.append(S.dma("act", out_d[b, rows, :].rearrange("(s p) d -> p s d", p=128), xt.h[:], reads=[xt.t]))
            S.barrier()

        order = [(b, j) for b in range(nb) for j in range(nt)]
        for idx, (b, j) in enumerate(order):
            if j == 0:
                MSET("pool", zp.h[:], 0.0, [zp])
                MSET("pool", halo.h[:], 0.0, [halo])
                for pp in range(4):
                    MSET("pool", Sst[pp].h[:], 0.0, [Sst[pp]])
            tile_body(b, j, idx, order[idx + 1] if idx + 1 < len(order) else None)
        S.final_wait("act", out_toks)
        S.emit()
    return nc


def _fm_blocks(W):
    K, N = W.shape
    nk = K // 128
    out = []
    for g in range(N // 512):
        blk = W[:, g * 512:(g + 1) * 512].reshape(nk, 128, 512).transpose(1, 0, 2).reshape(128, nk * 512)
        out.append(blk)
    return out


def _prep(inp):
    f = lambda a: np.asarray(a, dtype=np.float32)
    w_in = f(inp["w_in"])[0]
    cols = []
    cols.append(np.arange(1536, 1664))
    cols.append(np.arange(1664, 1792))
    for p in range(4):
        for base in (0, 512, 1024):
            cols.append(np.arange(base + p * 128, base + (p + 1) * 128))
    for g in range(4):
        cols.append(np.arange(1792 + g * 128, 1792 + (g + 1) * 128))
    for i in range(8):
        cols.append(np.arange(2304 + i * 128, 2304 + (i + 1) * 128))
        cols.append(np.arange(2304 + (8 + i) * 128, 2304 + (9 + i) * 128))
    cols = np.concatenate(cols)
    assert cols.shape[0] == 4352
    w_in_p = np.concatenate([w_in[:, cols], np.zeros((1024, 256), np.float32)], axis=1)
    blocks = []
    blocks += _fm_blocks(w_in_p)
    for nm in ("w_out_a", "w_out_b"):
        W = f(inp[nm])[0]
        blocks.append(W.reshape(4, 128, 1024).transpose(1, 0, 2).reshape(128, 4096))
    blocks += _fm_blocks(f(inp["w_o"])[0])
    w1 = f(inp["w_ff1"])[0]
    w2 = f(inp["w_ff2"])[0]
    for hh in range(2):
        blocks += _fm_blocks(w1[:, hh * 2048:(hh + 1) * 2048])
        for oh in range(2):
            for part in range(2):
                r0 = hh * 2048 + part * 1024
                sub = w2[r0:r0 + 1024, oh * 512:(oh + 1) * 512]
                blocks.append(sub.reshape(8, 128, 512).transpose(1, 0, 2).reshape(128, 4096))
    blocks += _fm_blocks(f(inp["w_ple_gate"])[0])
    wpe = f(inp["w_ple_proj"])[0]
    b31 = np.zeros((128, 4096), np.float32)
    b31[:, 0:2048] = wpe.reshape(2, 128, 1024).transpose(1, 0, 2).reshape(128, 2048)
    blocks.append(b31)
    b32 = np.zeros((128, 4096), np.float32)
    b32[0:64, 0:512] = f(inp["w_decay_up"])[0]
    b32[64:128, 512:1024] = f(inp["w_aaa_up"])[0]
    b32[:, 1024:1536] = f(inp["w_gate_up"])[0]
    b32[:, 1536:2048] = f(inp["pool_w"])[0].transpose(1, 0, 2).reshape(128, 512)
    blocks.append(b32)
    assert len(blocks) == NBLK
    wh = np.ascontiguousarray(np.stack(blocks, axis=0))

    pv = np.zeros((128, 62), np.float32)
    mu = f(inp["mu_shift"])[0]
    mu_cols = [mu[1536:1664], mu[1664:1792]]
    for p in range(4):
        for base in (0, 512, 1024):
            mu_cols.append(mu[base + p * 128: base + (p + 1) * 128])
    pv[:, V_MU:V_MU + 14] = np.stack(mu_cols, axis=1)
    for nm, col in (("w0", V_W0), ("a0", V_A0), ("k_k", V_KK), ("k_a", V_KA), ("ln_x_w", V_LW), ("ln_x_b", V_LB),
                    ("pool_scale", V_PS)):
        pv[:, col:col + 4] = f(inp[nm])[0].reshape(4, 128).T
    pv[:, V_RK:V_RK + 4] = f(inp["r_k"])[0].reshape(4, 128).T
    pv[:, V_BG:V_BG + 16] = f(inp["b_gates"])[0].reshape(16, 128).T
    gv = np.stack([f(inp["g_mix"])[0], f(inp["g_mlp"])[0], f(inp["g_ple"])[0], f(inp["g_final"])], axis=0)

    cst = np.zeros((128, NCST), np.float32)
    idx = np.arange(128)
    hb = idx // 64
    tt_ = idx % 64
    same = (hb[:, None] == hb[None, :]).astype(np.float32)
    cst[:, C_ID:C_ID + 128] = np.eye(128, dtype=np.float32)
    cst[:, C_BO64:C_BO64 + 128] = same / 64.0
    cst[:, C_BO:C_BO + 128] = same
    cst[:, C_ML:C_ML + 128] = -same * (tt_[None, :] < tt_[:, None])
    lt = same * (tt_[:, None] < tt_[None, :])
    le = same * (tt_[:, None] <= tt_[None, :])
    cst[:, C_MX:C_MX + 128] = -lt
    cst[:, C_MX + 128:C_MX + 256] = -le
    cst[:, C_MX + 256:C_MX + 384] = lt
    cst[:, C_MX + 384:C_MX + 512] = le
    rm = np.ones(512, np.float32)
    rm[::64] = 0.0
    cst[:, C_RM:C_RM + 512] = rm[None, :]
    for g in range(4):
        win = 2 << g
        cst[:, C_IC + g * 16:C_IC + (g + 1) * 16] = (1.0 / np.minimum(np.arange(16) + 1, win))[None, :]
    cst[:, C_NH] = -0.5
    cst[:, C_EPS] = 64e-5
    return wh, pv, gv, cst


_CACHE = {}


def kernel(**inputs):
    x = np.asarray(inputs["x"], dtype=np.float32)
    p = np.asarray(inputs["p"], dtype=np.float32)[0]
    B, T, D = x.shape
    ncores = 8
    nb = B // ncores
    nt = T // 512
    wh, pv, gv, cst = _prep(inputs)
    key = (nb, nt)
    if key not in _CACHE:
        _CACHE[key] = build(nb, nt)
    nc = _CACHE[key]
    in_maps = []
    for c in range(ncores):
        in_maps.append({"x": np.ascontiguousarray(x[c * nb:(c + 1) * nb]),
                        "p": np.ascontiguousarray(p[c * nb:(c + 1) * nb]),
                        "wh": wh, "pv": pv, "gv": gv, "cst": cst})
    res = run_bass_kernel_spmd(nc, in_maps, core_ids=list(range(ncores)))
    out = np.concatenate([np.asarray(r["out"]) for r in res.results], axis=0)
    return out.astype(np.float32)
```
